# Optimizing a Trainium2 kernel written in Bass

```python
import math
import jax, jax.numpy as jnp
from jax import lax
import numpy as np

D_MODEL = 1024
BATCH = 4
SEQ = 4096
DEPTH = 2
DEC_BATCH = 8
DEC_SEQ = 4096
PAST_LEN = 128

ROPE_THETA = 10000.0
EPS = 1e-6
NEG_INF = -1e30
QBLK = 128
N_BRANCH = 4

A_HEADS = 4
A_HD = 64
A_QK_W = A_HEADS * 2 * A_HD
A_V_W = A_HEADS * 2 * A_HD
A_OUT = A_V_W
B_PATTERNS = ((128, 1), (512, 4), (2048, 16))
B_HEADS = 4
B_HD = 64
B_BLK = 64
B_W = B_HEADS * B_HD
B_OUT = B_W
C_HEADS = 8
C_Q_RANK = 256
C_KV_RANK = 128
C_NOPE = 64
C_ROPE = 32
C_VD = 64
C_OUT = C_HEADS * C_VD
D_QHEADS = 8
D_KVHEADS = 2
D_HD = 64
D_WIN = 128
D_BLK = 128
D_Q_W = D_QHEADS * D_HD
D_KV_W = D_KVHEADS * D_HD
D_OUT = D_Q_W
D_FF = -(-8 * D_MODEL // (3 * 256)) * 256

IN_SIZES = (A_QK_W, A_QK_W, A_V_W) + (B_W,) * (3 * len(B_PATTERNS)) + (C_Q_RANK, C_KV_RANK, C_ROPE, D_Q_W, D_KV_W, D_KV_W)
N_IN = sum(IN_SIZES)
SPLIT_IDX = tuple(int(i) for i in np.cumsum(IN_SIZES)[:-1])

kernel_name = "hybrid_gated_encoder_4mixer"


def lambda_init(layer):
    return 0.8 - 0.6 * math.exp(-0.3 * layer)


def rms_norm(x, g):
    xf = x.astype(jnp.float32)
    y = xf * lax.rsqrt(jnp.mean(xf * xf, axis=-1, keepdims=True) + EPS)
    return (y * g.astype(jnp.float32)).astype(x.dtype)


def rope(x, pos):
    half = x.shape[-1] // 2
    inv = jnp.power(ROPE_THETA, -jnp.arange(half, dtype=jnp.float32) / half)
    ang = pos.astype(jnp.float32)[:, None] * inv[None, :]
    cos = jnp.cos(ang)[None, :, None, :]
    sin = jnp.sin(ang)[None, :, None, :]
    xf = x.astype(jnp.float32)
    x1, x2 = xf[..., :half], xf[..., half:]
    return jnp.concatenate([x1 * cos - x2 * sin, x2 * cos + x1 * sin], axis=-1).astype(x.dtype)


def _qblocks(t):
    b, s = t.shape[:2]
    return jnp.moveaxis(t.reshape(b, s // QBLK, QBLK, *t.shape[2:]), 1, 0)


def _unblock(t):
    t = jnp.moveaxis(t, 0, 1)
    return t.reshape(t.shape[0], -1, *t.shape[3:])


def _to_residues(t, dil):
    b, s = t.shape[:2]
    return t.reshape(b, s // dil, dil, *t.shape[2:]).swapaxes(1, 2).reshape(b * dil, s // dil, *t.shape[2:])


def _from_residues(t, dil, b):
    n, m = t.shape[:2]
    return t.reshape(b, dil, m, *t.shape[2:]).swapaxes(1, 2).reshape(b, dil * m, *t.shape[2:])


def banded_attention(q, k, v, half_w, blk, sink=None):
    n, L, hq, d = q.shape
    hk = k.shape[2]
    rep = hq // hk
    nblk = -(-L // blk)
    lp = nblk * blk
    nb = -(-half_w // blk)
    kw_len = (2 * nb + 1) * blk
    q = jnp.pad(q, ((0, 0), (0, lp - L), (0, 0), (0, 0)))
    kv_pad = ((0, 0), (nb * blk, lp - L + nb * blk), (0, 0), (0, 0))
    kb = jnp.pad(k, kv_pad).reshape(n, nblk + 2 * nb, blk, hk, d)
    vb = jnp.pad(v, kv_pad).reshape(n, nblk + 2 * nb, blk, hk, d)
    kw = jnp.concatenate([kb[:, j:j + nblk] for j in range(2 * nb + 1)], axis=2)
    vw = jnp.concatenate([vb[:, j:j + nblk] for j in range(2 * nb + 1)], axis=2)
    qb = q.reshape(n, nblk, blk, hk, rep, d)
    s = jnp.einsum("nbqgrd,nbkgd->nbgrqk", qb, kw).astype(jnp.float32) * (d ** -0.5)
    qpos = jnp.arange(lp).reshape(nblk, blk)
    kpos = (jnp.arange(nblk)[:, None] - nb) * blk + jnp.arange(kw_len)[None, :]
    valid = (jnp.abs(qpos[:, :, None] - kpos[:, None, :]) <= half_w) & ((kpos >= 0) & (kpos < L))[:, None, :]
    s = jnp.where(valid[None, :, None, None], s, NEG_INF)
    m = jnp.max(s, axis=-1, keepdims=True)
    if sink is not None:
        sk = sink.astype(jnp.float32).reshape(1, 1, hk, rep, 1, 1)
        m = jnp.maximum(m, sk)
        e = jnp.exp(s - m)
        l = jnp.sum(e, axis=-1, keepdims=True) + jnp.exp(sk - m)
    else:
        e = jnp.exp(s - m)
        l = jnp.sum(e, axis=-1, keepdims=True)
    p = e / l
    o = jnp.einsum("nbgrqk,nbkgd->nbqgrd", p.astype(v.dtype), vw).reshape(n, lp, hq, d)[:, :L]
    lse = (m + jnp.log(l))[..., 0]
    lse = jnp.moveaxis(lse, 4, 2).reshape(n, lp, hq)[:, :L]
    return o, lse


def diff_attention(a_q, a_k, a_v, qn_g, kn_g, lam_p, subln_g, lam_init, pos):
    b, s, _ = a_q.shape
    q = rope(rms_norm(a_q.reshape(b, s, 2 * A_HEADS, A_HD), qn_g), pos).reshape(b, s, A_HEADS, 2, A_HD)
    k = rope(rms_norm(a_k.reshape(b, s, 2 * A_HEADS, A_HD), kn_g), pos).reshape(b, s, A_HEADS, 2, A_HD)
    v = a_v.reshape(b, s, A_HEADS, 2 * A_HD)
    lp = lam_p.astype(jnp.float32)
    lam = jnp.exp(jnp.sum(lp[0] * lp[1])) - jnp.exp(jnp.sum(lp[2] * lp[3])) + lam_init

    def block(qb):
        sc = jnp.einsum("bqhcd,bkhcd->bhcqk", qb, k).astype(jnp.float32) * (A_HD ** -0.5)
        p = jax.nn.softmax(sc, axis=-1)
        w = p[:, :, 0] - lam * p[:, :, 1]
        return jnp.einsum("bhqk,bkhe->bqhe", w.astype(v.dtype), v)

    o = _unblock(lax.map(block, _qblocks(q)))
    o = rms_norm(o, subln_g) * (1.0 - lam_init)
    return o.reshape(b, s, A_OUT)


def dilated_attention(cols, qn_g, kn_g, pos):
    b, s, _ = cols[0].shape
    outs, lses = [], []
    for g, (window, dil) in enumerate(B_PATTERNS):
        q, k, v = (c.reshape(b, s, B_HEADS, B_HD) for c in cols[3 * g:3 * g + 3])
        q = rope(rms_norm(q, qn_g[g]), pos)
        k = rope(rms_norm(k, kn_g[g]), pos)
        o, lse = banded_attention(_to_residues(q, dil), _to_residues(k, dil), _to_residues(v, dil),
                                  window // (2 * dil), B_BLK)
        outs.append(_from_residues(o, dil, b))
        lses.append(_from_residues(lse, dil, b))
    wts = jax.nn.softmax(jnp.stack(lses, axis=0), axis=0)
    out = jnp.einsum("gbsh,gbshd->bshd", wts, jnp.stack(outs, axis=0).astype(jnp.float32))
    return out.astype(cols[0].dtype).reshape(b, s, B_OUT)


def mla_attention(c_q, c_kv, k_rope, qa_g, kva_g, w_uq, w_ukv, qn_g, kn_g, pos):
    b, s, _ = c_q.shape
    q = (rms_norm(c_q, qa_g) @ w_uq).reshape(b, s, C_HEADS, C_NOPE + C_ROPE)
    kv = (rms_norm(c_kv, kva_g) @ w_ukv).reshape(b, s, C_HEADS, C_NOPE + C_VD)
    v = kv[..., C_NOPE:]
    k = jnp.concatenate([kv[..., :C_NOPE], jnp.broadcast_to(k_rope[:, :, None, :], (b, s, C_HEADS, C_ROPE))], axis=-1)
    q = rms_norm(q, qn_g)
    k = rms_norm(k, kn_g)
    q = jnp.concatenate([q[..., :C_NOPE], rope(q[..., C_NOPE:], pos)], axis=-1)
    k = jnp.concatenate([k[..., :C_NOPE], rope(k[..., C_NOPE:], pos)], axis=-1)
    scale = (C_NOPE + C_ROPE) ** -0.5

    def block(qb):
        sc = jnp.einsum("bqhd,bkhd->bhqk", qb, k).astype(jnp.float32) * scale
        p = jax.nn.softmax(sc, axis=-1)
        return jnp.einsum("bhqk,bkhe->bqhe", p.astype(v.dtype), v)

    o = _unblock(lax.map(block, _qblocks(q)))
    return o.reshape(b, s, C_OUT)


def window_gqa_sink(d_q, d_k, d_v, qn_g, kn_g, sink, pos):
    b, s, _ = d_q.shape
    q = rope(rms_norm(d_q.reshape(b, s, D_QHEADS, D_HD), qn_g), pos)
    k = rope(rms_norm(d_k.reshape(b, s, D_KVHEADS, D_HD), kn_g), pos)
    v = d_v.reshape(b, s, D_KVHEADS, D_HD)
    o, _ = banded_attention(q, k, v, D_WIN, D_BLK, sink)
    return o.reshape(b, s, D_OUT)


def encoder_layer(x, lam_init, norm1_g, w_in, w_gate, a_qnorm_g, a_knorm_g, a_lambda, a_subln_g,
                  b_qnorm_g, b_knorm_g, c_qa_norm_g, c_kva_norm_g, c_w_uq, c_w_ukv, c_qnorm_g, c_knorm_g,
                  d_qnorm_g, d_knorm_g, d_sink, w_br_a, w_br_b, w_br_c, w_br_d, w_o,
                  norm2_g, w_ffn_gate, w_ffn_up, w_ffn_down):
    b, s, _ = x.shape
    pos = jnp.arange(s)
    h = rms_norm(x, norm1_g)
    cols = jnp.split(h @ w_in, SPLIT_IDX, axis=-1)
    y_a = diff_attention(cols[0], cols[1], cols[2], a_qnorm_g, a_knorm_g, a_lambda, a_subln_g, lam_init, pos) @ w_br_a
    y_b = dilated_attention(cols[3:12], b_qnorm_g, b_knorm_g, pos) @ w_br_b
    y_c = mla_attention(cols[12], cols[13], cols[14], c_qa_norm_g, c_kva_norm_g, c_w_uq, c_w_ukv,
                        c_qnorm_g, c_knorm_g, pos) @ w_br_c
    y_d = window_gqa_sink(cols[15], cols[16], cols[17], d_qnorm_g, d_knorm_g, d_sink, pos) @ w_br_d
    gates = jax.nn.sigmoid((h @ w_gate).astype(jnp.float32)).reshape(b, s, N_BRANCH, D_MODEL)
    branches = jnp.stack([y_a, y_b, y_c, y_d], axis=2).astype(jnp.float32)
    merged = jnp.sum(gates * branches, axis=2).astype(x.dtype)
    x = x + merged @ w_o
    hf = rms_norm(x, norm2_g)
    x = x + (jax.nn.silu(hf @ w_ffn_gate) * (hf @ w_ffn_up)) @ w_ffn_down
    return x


def _trunk(x, params):
    for l in range(DEPTH):
        x = encoder_layer(x, lambda_init(l), *[p[l] for p in params])
    return x


def setup_inputs(seed: int = 0) -> dict:
    key = jax.random.key(seed)
    ks = iter(jax.random.split(key, 40))

    def nrm(shape, scale):
        return jax.random.normal(next(ks), shape, jnp.float32) * scale

    def gain(shape):
        return 1.0 + nrm(shape, 0.02)

    L = DEPTH
    return {
        "x_prompt": nrm((BATCH, SEQ, D_MODEL), 1.0),
        "x_sample": nrm((DEC_BATCH, DEC_SEQ, D_MODEL), 1.0),
        "norm1_g": gain((L, D_MODEL)),
        "w_in": nrm((L, D_MODEL, N_IN), D_MODEL ** -0.5),
        "w_gate": nrm((L, D_MODEL, N_BRANCH * D_MODEL), D_MODEL ** -0.5),
        "a_qnorm_g": gain((L, A_HD)),
        "a_knorm_g": gain((L, A_HD)),
        "a_lambda": nrm((L, 4, A_HD), 0.1),
        "a_subln_g": gain((L, 2 * A_HD)),
        "b_qnorm_g": gain((L, len(B_PATTERNS), B_HD)),
        "b_knorm_g": gain((L, len(B_PATTERNS), B_HD)),
        "c_qa_norm_g": gain((L, C_Q_RANK)),
        "c_kva_norm_g": gain((L, C_KV_RANK)),
        "c_w_uq": nrm((L, C_Q_RANK, C_HEADS * (C_NOPE + C_ROPE)), C_Q_RANK ** -0.5),
        "c_w_ukv": nrm((L, C_KV_RANK, C_HEADS * (C_NOPE + C_VD)), C_KV_RANK ** -0.5),
        "c_qnorm_g": gain((L, C_NOPE + C_ROPE)),
        "c_knorm_g": gain((L, C_NOPE + C_ROPE)),
        "d_qnorm_g": gain((L, D_HD)),
        "d_knorm_g": gain((L, D_HD)),
        "d_sink": nrm((L, D_QHEADS), 0.5),
        "w_br_a": nrm((L, A_OUT, D_MODEL), A_OUT ** -0.5),
        "w_br_b": nrm((L, B_OUT, D_MODEL), B_OUT ** -0.5),
        "w_br_c": nrm((L, C_OUT, D_MODEL), C_OUT ** -0.5),
        "w_br_d": nrm((L, D_OUT, D_MODEL), D_OUT ** -0.5),
        "w_o": nrm((L, D_MODEL, D_MODEL), D_MODEL ** -0.5),
        "norm2_g": gain((L, D_MODEL)),
        "w_ffn_gate": nrm((L, D_MODEL, D_FF), D_MODEL ** -0.5),
        "w_ffn_up": nrm((L, D_MODEL, D_FF), D_MODEL ** -0.5),
        "w_ffn_down": nrm((L, D_FF, D_MODEL), D_FF ** -0.5),
    }


def reference(x_prompt, x_sample, norm1_g, w_in, w_gate, a_qnorm_g, a_knorm_g, a_lambda, a_subln_g,
              b_qnorm_g, b_knorm_g, c_qa_norm_g, c_kva_norm_g, c_w_uq, c_w_ukv, c_qnorm_g, c_knorm_g,
              d_qnorm_g, d_knorm_g, d_sink, w_br_a, w_br_b, w_br_c, w_br_d, w_o,
              norm2_g, w_ffn_gate, w_ffn_up, w_ffn_down):
    params = (norm1_g, w_in, w_gate, a_qnorm_g, a_knorm_g, a_lambda, a_subln_g,
              b_qnorm_g, b_knorm_g, c_qa_norm_g, c_kva_norm_g, c_w_uq, c_w_ukv, c_qnorm_g, c_knorm_g,
              d_qnorm_g, d_knorm_g, d_sink, w_br_a, w_br_b, w_br_c, w_br_d, w_o,
              norm2_g, w_ffn_gate, w_ffn_up, w_ffn_down)
    y_prompt = _trunk(x_prompt, params)
    y_sample = _trunk(x_sample, params)
    return (y_prompt, y_sample)
```

```python
import math
from collections import deque
from contextlib import ExitStack

import numpy as np
import ml_dtypes

import concourse.bass as bass
import concourse.mybir as mybir
from concourse.bass_utils import run_bass_kernel_spmd

F32, BF16 = mybir.dt.float32, mybir.dt.bfloat16
AF = mybir.ActivationFunctionType
ALU = mybir.AluOpType
AX = mybir.AxisListType

D = 1024
NIN = 5024
DFF = 2816
EPS = 1e-6
NQK = 41
VW = 1920
NOC = 14
import os as _os
VARIANT = int(_os.environ.get('PA_VARIANT', '1'))
MASK_DEF = {"d": (512, 1152, 128, 1), "b0": (512, 1152, 64, 1), "b1": (640, 1408, 256, 4), "b2": (1408, 2944, 1024, 16)}
MASK_COL = {}
_c = 0
for _k in ("d", "b0", "b1", "b2"):
    MASK_COL[_k] = _c
    _c += MASK_DEF[_k][1]
MASK_W = _c


def lambda_init(layer):
    return 0.8 - 0.6 * math.exp(-0.3 * layer)


class Buf:
    __slots__ = ("name", "w", "r", "dsem", "psum")

    def __init__(self, name, psum=False):
        self.name = name
        self.w = None
        self.r = {}
        self.dsem = None
        self.psum = psum


def PBuf(name):
    return Buf(name, psum=True)


class Tracker:
    def __init__(self, nc, es, ndma=60):
        self.nc = nc
        self.eng = {"pe": nc.tensor, "act": nc.scalar, "dve": nc.vector, "pool": nc.gpsimd, "sp": nc.sync}
        self.sems, self.val = [], []
        self.esem = {}
        for e in self.eng:
            self.esem[e] = self._new(es, "e_" + e)
        self.free_dma = [self._new(es, "d%d" % i) for i in range(ndma)]
        self.used_dma = []
        self.waited = {e: {} for e in self.eng}
        self.pending = {e: False for e in self.eng}
        self.n = 0
        self.log = None

    def _new(self, es, name):
        self.sems.append(es.enter_context(self.nc.semaphore(name)))
        self.val.append(0)
        return len(self.sems) - 1

    def _deps(self, reads, writes):
        raw, oth = {}, {}
        for b in reads:
            if b.w is not None:
                k, v = b.w
                if raw.get(k, 0) < v:
                    raw[k] = v
            if b.psum:
                for k, v in b.r.items():
                    if oth.get(k, 0) < v:
                        oth[k] = v
        for b in writes:
            if b.w is not None:
                k, v = b.w
                if oth.get(k, 0) < v:
                    oth[k] = v
            for k, v in b.r.items():
                if oth.get(k, 0) < v:
                    oth[k] = v
        return raw, oth

    def _wait(self, e, raw, oth):
        own = self.esem[e]
        w = self.waited[e]
        for k, v in raw.items():
            if k == own:
                if e == "pe":
                    continue
                assert v <= self.val[own], "own pending dep"
            if w.get(k, 0) < v:
                self.eng[e].wait_ge(self.sems[k], v)
                w[k] = v
                if self.log is not None:
                    self.log[e].append(("w", k, v))
        for k, v in oth.items():
            if k == own:
                continue
            if w.get(k, 0) < v:
                self.eng[e].wait_ge(self.sems[k], v)
                w[k] = v
                if self.log is not None:
                    self.log[e].append(("w", k, v))

    def op(self, e, fn, reads=(), writes=(), inc=True):
        raw, oth = self._deps(reads, writes)
        self._wait(e, raw, oth)
        ins = fn()
        self.n += 1
        k = self.esem[e]
        if inc:
            self.val[k] += 1
            ins.then_inc(self.sems[k], 1)
            if self.log is not None:
                self.log[e].append(("i", k, 1))
            st = self.val[k]
            self.pending[e] = False
        else:
            st = self.val[k] + 1
            self.pending[e] = True
        for b in reads:
            if b.r.get(k, 0) < st:
                b.r[k] = st
        for b in writes:
            b.w = (k, st)
            b.r = {}
        return ins

    def dma(self, q, out, in_, sb, reads=(), writes=()):
        raw, oth = self._deps(reads, writes)
        self._wait(q, raw, oth)
        if sb.dsem is None:
            sb.dsem = self.free_dma.pop()
            self.used_dma.append(sb.dsem)
        k = sb.dsem
        self.val[k] += 16
        self.eng[q].dma_start(out=out, in_=in_).then_inc(self.sems[k], 16)
        if self.log is not None:
            self.log[q].append(("i", k, 16))
        self.n += 1
        st = self.val[k]
        for b in reads:
            if b.r.get(k, 0) < st:
                b.r[k] = st
        for b in writes:
            b.w = (k, st)
            b.r = {}

    def barrier(self):
        for e in self.eng:
            assert not self.pending[e], e
        ks = list(self.esem.values()) + self.used_dma
        for e in self.eng:
            own = self.esem[e]
            w = self.waited[e]
            for k in ks:
                if k == own:
                    continue
                v = self.val[k]
                if v > 0 and w.get(k, 0) < v:
                    self.eng[e].wait_ge(self.sems[k], v)
                    w[k] = v
                    if self.log is not None:
                        self.log[e].append(("w", k, v))
        self.free_dma.extend(self.used_dma)
        self.used_dma = []


class Builder:
    def __init__(self, S=4096, NSEQ=2, NL=2, debug=False, phases=None):
        self.S, self.NSEQ, self.NL = S, NSEQ, NL
        self.NT = S // 128
        self.NQT = S // 512
        self.debug = debug
        self.phases = phases
        self.uid = 0
        nc = self.nc = bass.Bass("TRN2", target_bir_lowering=False)
        dt = nc.dram_tensor
        L = 2
        self.x = dt("x", [NSEQ, S, D], F32, kind="ExternalInput").ap()
        self.y = dt("y", [NSEQ, S, D], F32, kind="ExternalOutput").ap()
        W = {}
        for name, shape in [
            ("norm1_g", [L, D]), ("w_in", [L, D, NIN]), ("w_gate", [L, D, 4 * D]),
            ("a_qnorm_g", [L, 64]), ("a_knorm_g", [L, 64]), ("a_lambda", [L, 4, 64]), ("a_subln_g", [L, 128]),
            ("b_qnorm_g", [L, 3, 64]), ("b_knorm_g", [L, 3, 64]), ("c_qa_norm_g", [L, 256]),
            ("c_kva_norm_g", [L, 128]), ("c_w_uq", [L, 256, 768]), ("c_w_ukv", [L, 128, 1024]),
            ("c_qnorm_g", [L, 96]), ("c_knorm_g", [L, 96]), ("d_qnorm_g", [L, 64]), ("d_knorm_g", [L, 64]),
            ("d_sink", [L, 8]), ("w_br_a", [L, 512, D]), ("w_br_b", [L, 256, D]), ("w_br_c", [L, 512, D]),
            ("w_br_d", [L, 512, D]), ("w_o", [L, D, D]), ("norm2_g", [L, D]), ("w_ffn_gate", [L, D, DFF]),
            ("w_ffn_up", [L, D, DFF]), ("w_ffn_down", [L, DFF, D]),
        ]:
            W[name] = dt(name, shape, F32, kind="ExternalInput").ap()
        self.W = W
        self.ident_d = dt("ident", [128, 128], BF16, kind="ExternalInput").ap()
        self.cs32_d = dt("cs32", [S, 64], F32, kind="ExternalInput").ap()
        self.cs16_d = dt("cs16", [S, 32], F32, kind="ExternalInput").ap()
        self.masks_d = dt("masks", [128, MASK_W], BF16, kind="ExternalInput").ap()
        sk = "ExternalOutput" if debug else "Internal"
        self.hT_d = dt("hT_s", [NSEQ, 8, 128, S], BF16, kind=sk).ap()
        self.qk_d = dt("qk_s", [NSEQ, NQK, 128, S], BF16, kind=sk).ap()
        self.v_d = dt("v_s", [NSEQ, S, VW], BF16, kind=sk).ap()
        self.oT_d = dt("oT_s", [NSEQ, NOC, 128, S], BF16, kind=sk).ap()
        self.mg_d = dt("mg_s", [NSEQ, 8, 128, S], BF16, kind=sk).ap()
        self.aT_d = dt("aT_s", [NSEQ, 22, 128, S], BF16, kind=sk).ap()
        self.xm_d = dt("xm_s", [NSEQ, S, D], F32, kind=sk).ap()
        self.x1_d = dt("x1_s", [NSEQ, S, D], F32, kind=sk).ap()
        with ExitStack() as es:
            self.tr = Tracker(nc, es)
            if debug == "log":
                self.tr.log = {e: [] for e in self.tr.eng}
            self.build()

    def nm(self, s):
        self.uid += 1
        return "%s_%d" % (s, self.uid)

    def build(self):
        tr = self.tr
        pa_pre = None
        for l in range(self.NL):
            xin = self.x if l == 0 else self.x1_d
            xout = self.y if l == self.NL - 1 else self.x1_d
            if self.phases is not None:
                for ph, fn in [("PA", lambda: self.phase_PA(l, xin)), ("A", lambda: self.phase_A(l)),
                               ("C", lambda: self.phase_64(l, "c")), ("D", lambda: self.phase_64(l, "d")),
                               ("B", lambda: self.phase_64(l, "b")), ("M1", lambda: self.phase_M1g(l)),
                               ("MO", lambda: self.phase_MO(l, xin)), ("M2", lambda: self.phase_M2d(l, xout))]:
                    if ph in self.phases:
                        fn()
                        tr.barrier()
                continue
            self.phase_PA(l, xin, pre=pa_pre)
            tr.barrier()
            if pa_pre is not None:
                pa_pre[0].close()
                pa_pre = None
            self.phase_A(l); tr.barrier()
            self.phase_64(l, "c"); tr.barrier()
            self.phase_64(l, "b"); tr.barrier()
            with ExitStack() as esw:
                m1w = self.load_m1g_weights(esw, l)
                self.phase_64(l, "d"); tr.barrier()
                self.phase_M1g(l, pre=m1w); tr.barrier()
            self.phase_MO(l, xin); tr.barrier()
            if l + 1 < self.NL:
                esn = ExitStack()
                pa_pre = (esn,) + self.load_pa_weights(esn, l + 1)
            self.phase_M2d(l, xout); tr.barrier()
        tr.barrier()

    def load_pa_weights(self, es, l):
        nc, tr, W = self.nc, self.tr, self.W
        w_in = es.enter_context(nc.sbuf_tensor(self.nm("w_in"), [128, 8, NIN], BF16)); w_in_b = Buf("w_in")
        w_uq = es.enter_context(nc.sbuf_tensor(self.nm("w_uq"), [128, 2, 768], BF16))
        w_ukv = es.enter_context(nc.sbuf_tensor(self.nm("w_ukv"), [128, 1024], BF16)); w_c_b = Buf("w_c")
        for kc in range(8):
            tr.dma("pool", w_in[:, kc, :], W["w_in"][l, kc * 128:(kc + 1) * 128, :], w_in_b, writes=[w_in_b])
        for kc in range(2):
            tr.dma("pool", w_uq[:, kc, :], W["c_w_uq"][l, kc * 128:(kc + 1) * 128, :], w_c_b, writes=[w_c_b])
        tr.dma("pool", w_ukv[:], W["c_w_ukv"][l], w_c_b, writes=[w_c_b])
        return (w_in, w_in_b, w_uq, w_ukv, w_c_b)

    def load_m1g_weights(self, es, l):
        nc, tr, W = self.nc, self.tr, self.W
        wg = es.enter_context(nc.sbuf_tensor(self.nm("wg"), [128, 8, 4 * D], BF16)); wg_b = Buf("wg")
        wbr = es.enter_context(nc.sbuf_tensor(self.nm("wbr"), [128, NOC, D], BF16)); wbr_b = Buf("wbr")
        for kc in range(8):
            tr.dma("pool", wg[:, kc, :], W["w_gate"][l, kc * 128:(kc + 1) * 128, :], wg_b, writes=[wg_b])
        ci = 0
        br_chunks = []
        for name, n in (("w_br_a", 4), ("w_br_b", 2), ("w_br_c", 4), ("w_br_d", 4)):
            tr.dma("pool", wbr[:, ci:ci + n, :], W[name][l].rearrange("(n p) d -> p n d", p=128), wbr_b, writes=[wbr_b])
            br_chunks.append(list(range(ci, ci + n)))
            ci += n
        return (wg, wg_b, wbr, wbr_b, br_chunks)

    def phase_PA(self, l, xin, pre=None):
        nc, tr, W = self.nc, self.tr, self.W
        S, NT, NSEQ = self.S, self.NT, self.NSEQ
        op, dma = tr.op, tr.dma
        with ExitStack() as es:
            def sb(name, shape, dt):
                return es.enter_context(nc.sbuf_tensor(self.nm(name), shape, dt))

            def ps(name, shape, dt):
                return es.enter_context(nc.psum_tensor(self.nm(name), shape, dt))

            if pre is not None:
                _, w_in, w_in_b, w_uq, w_ukv, w_c_b = pre
            else:
                w_in, w_in_b, w_uq, w_ukv, w_c_b = self.load_pa_weights(es, l)
            GOFF = {}
            goff = 0
            glist = [("g1", W["norm1_g"][l], 1024), ("a_q", W["a_qnorm_g"][l], 64), ("a_k", W["a_knorm_g"][l], 64),
                     ("b_q", W["b_qnorm_g"][l].rearrange("g d -> (g d)"), 192),
                     ("b_k", W["b_knorm_g"][l].rearrange("g d -> (g d)"), 192),
                     ("c_qa", W["c_qa_norm_g"][l], 256), ("c_kva", W["c_kva_norm_g"][l], 128),
                     ("c_q", W["c_qnorm_g"][l], 96), ("c_k", W["c_knorm_g"][l], 96),
                     ("d_q", W["d_qnorm_g"][l], 64), ("d_k", W["d_knorm_g"][l], 64)]
            GWT = sum(g[2] for g in glist)
            gains = sb("gains", [128, GWT], F32); gains_b = Buf("gains")
            for name, ap, w in glist:
                GOFF[name] = goff
                dma("sp", gains[:, goff:goff + w], ap.partition_broadcast(128), gains_b, writes=[gains_b])
                goff += w

            def G(name, off, w):
                o = GOFF[name] + off
                return gains[:, o:o + w]

            cs32 = sb("cs32", [128, NT, 64], F32)
            cs16 = sb("cs16", [128, NT, 32], F32)
            ident = sb("ident", [128, 128], BF16)
            const_b = Buf("const")
            dma("sp", cs32[:], self.cs32_d.rearrange("(n p) f -> p n f", p=128), const_b, writes=[const_b])
            dma("sp", cs16[:], self.cs16_d.rearrange("(n p) f -> p n f", p=128), const_b, writes=[const_b])
            dma("sp", ident[:], self.ident_d, const_b, writes=[const_b])

            xt = [sb("xt", [128, D], F32) for _ in range(2)]; xt_b = [Buf("xt") for _ in range(2)]
            junk = sb("junk", [128, D], BF16); junk_b = Buf("junk")
            hx = None; hx_b = Buf("hx")
            st1 = [sb("st1", [128, 4], F32) for _ in range(2)]; st1_b = [Buf("st1") for _ in range(2)]
            hb = [sb("hb", [128, D], BF16) for _ in range(2)]; hb_b = [Buf("hb") for _ in range(2)]
            hT = [sb("hT", [128, 8, 128], BF16) for _ in range(2)]; hT_b = [Buf("hT") for _ in range(2)]
            NW = 6
            sq = [sb("sq", [128, 768], F32) for _ in range(2)]; sq_b = [Buf("sq") for _ in range(2)]
            stt = [sb("stt", [128, 32], F32) for _ in range(NW)]; stt_b = [Buf("stt") for _ in range(NW)]
            tt = [sb("tt", [128, 768], F32) for _ in range(NW)]; tt_b = [Buf("tt") for _ in range(NW)]
            rp = [sb("rp", [128, 4, 256], F32) for _ in range(2)]; rp_b = [Buf("rpa") for _ in range(2)]; rpb_b = [Buf("rpb") for _ in range(2)]
            QNW = 4736 + 1536
            qn = [sb("qn", [128, QNW], BF16) for _ in range(2)]; qn_b = [[Buf("qn%d" % i) for i in range(16)] for _ in range(2)]
            vst = [sb("vst", [128, VW], BF16) for _ in range(2)]; vst_b = [Buf("vst") for _ in range(2)]
            qTs = [sb("qTs", [128, 4, 128], BF16) for _ in range(3)]; qTs_b = [Buf("qTs") for _ in range(3)]
            cqn = [sb("cqn", [128, 384], BF16) for _ in range(2)]; cqn_b = [Buf("cqn") for _ in range(2)]
            cT = [sb("cT", [128, 3, 128], BF16) for _ in range(2)]; cT_b = [Buf("cT") for _ in range(2)]
            kcb = [sb("kcb", [128, 8, 96], F32) for _ in range(2)]; kcb_b = [Buf("kcb") for _ in range(2)]
            krp = [sb("krp", [128, 32], F32) for _ in range(2)]; krp_b = [Buf("krp") for _ in range(2)]

            NPJ = 4
            pj = [ps("pj", [128, 512], F32) for _ in range(NPJ)]; pj_b = [PBuf("pj") for _ in range(NPJ)]
            ptr = [ps("ptr", [128, 8, 128], BF16) for _ in range(2)]; ptr_b = [PBuf("ptr") for _ in range(2)]
            pq = ps("pq", [128, 1024], F32); pq_b = PBuf("pq")
            pkv = pq; pkv_b = pq_b

            cnt = {"w": 0, "tr": 0, "qs": 0}

            jobs = []

            def advance():
                for jb in reversed(jobs):
                    jb[1][jb[0]]()
                    jb[0] += 1
                while jobs and jobs[0][0] >= 4:
                    jobs.pop(0)

            def submit(src3, src_bufs, H, dh, g_ap, rope_spec, dst3, dst_bufs, after=None):
                wi = cnt["w"] % NW; cnt["w"] += 1
                w2 = wi % 2
                Wd = H * dh
                t3 = tt[wi][:, 0:Wd].rearrange("p (h d) -> p h d", h=H)
                sq3 = sq[w2][:, 0:Wd].rearrange("p (h d) -> p h d", h=H)
                tb, sb_, stb = tt_b[wi], sq_b[w2], stt_b[wi]
                op("act", lambda: nc.scalar.activation(out=sq3, in_=src3, func=AF.Square), reads=src_bufs, writes=[sb_])
                op("dve", lambda: nc.vector.tensor_tensor(out=t3, in0=src3, in1=g_ap.unsqueeze(1).to_broadcast([128, H, dh]), op=ALU.mult),
                   reads=list(src_bufs) + [gains_b], writes=[tb])
                if rope_spec is not None:
                    off, half, cos_ap, sin_ap = rope_spec
                    x1 = t3[:, :, off:off + half]
                    x2 = t3[:, :, off + half:off + 2 * half]
                    cb = cos_ap.unsqueeze(1).to_broadcast([128, H, half])
                    sbb = sin_ap.unsqueeze(1).to_broadcast([128, H, half])
                    n = H * half
                    r = rp[w2]
                    ra = r[:, 0, 0:n].rearrange("p (h d) -> p h d", h=H)
                    rb = r[:, 1, 0:n].rearrange("p (h d) -> p h d", h=H)
                    rc = r[:, 2, 0:n].rearrange("p (h d) -> p h d", h=H)
                    rd = r[:, 3, 0:n].rearrange("p (h d) -> p h d", h=H)
                    pa, pb = rp_b[w2], rpb_b[w2]

                def S1():
                    op("dve", lambda: nc.vector.tensor_reduce(out=stt[wi][:, 0:H], in_=sq3, axis=AX.X, op=ALU.add), reads=[sb_], writes=[stb])
                    if rope_spec is not None:
                        op("pool", lambda: nc.gpsimd.tensor_tensor(out=ra, in0=x1, in1=cb, op=ALU.mult), reads=[tb, const_b], writes=[pa])
                        op("dve", lambda: nc.vector.tensor_tensor(out=rb, in0=x2, in1=sbb, op=ALU.mult), reads=[tb, const_b], writes=[pb])
                        op("pool", lambda: nc.gpsimd.tensor_tensor(out=rd, in0=x1, in1=sbb, op=ALU.mult), reads=[tb, const_b], writes=[pa])
                        op("dve", lambda: nc.vector.tensor_tensor(out=rc, in0=x2, in1=cb, op=ALU.mult), reads=[tb, const_b], writes=[pb])

                def S2():
                    op("act", lambda: nc.scalar.activation(out=stt[wi][:, 8:8 + H], in_=stt[wi][:, 0:H], func=AF.Sqrt, bias=EPS, scale=1.0 / dh),
                       reads=[stb], writes=[stb])
                    if rope_spec is not None:
                        op("pool", lambda: nc.gpsimd.tensor_tensor(out=x1, in0=ra, in1=rb, op=ALU.subtract), reads=[pa, pb], writes=[tb])
                        op("dve", lambda: nc.vector.tensor_tensor(out=x2, in0=rc, in1=rd, op=ALU.add), reads=[pa, pb], writes=[tb])

                def S3():
                    op("dve", lambda: nc.vector.reciprocal(out=stt[wi][:, 16:16 + H], in_=stt[wi][:, 8:8 + H]), reads=[stb], writes=[stb])

                def S4():
                    if H > 2 and cnt["w"] % 2 == 0:
                        op("pool", lambda: nc.gpsimd.tensor_tensor(out=dst3, in0=t3, in1=stt[wi][:, 16:16 + H].unsqueeze(2).to_broadcast([128, H, dh]), op=ALU.mult),
                           reads=[tb, stb], writes=dst_bufs)
                    else:
                        for h in range(H):
                            op("act", lambda h=h: nc.scalar.activation(out=dst3[:, h, :], in_=t3[:, h, :], func=AF.Copy, scale=stt[wi][:, 16 + h:17 + h]),
                               reads=[tb, stb], writes=dst_bufs)
                    if after is not None:
                        after()

                advance()
                jobs.append([0, [S1, S2, S3, S4]])

            def transpose_out(s, t, src_tile, src_buf, col0, width, nchunk, chunk0):
                done = 0
                while done < nchunk:
                    n = min(4, nchunk - done)
                    pi = cnt["tr"] % 2; cnt["tr"] += 1
                    qi = cnt["qs"] % 3; cnt["qs"] += 1
                    for j in range(n):
                        c = col0 + (done + j) * width
                        op("pe", lambda j=j, c=c: nc.tensor.transpose(ptr[pi][0:width, j, :], src_tile[:, c:c + width], ident[:]),
                           reads=[src_buf, const_b], writes=[ptr_b[pi]], inc=(j == n - 1))
                    op("act", lambda: nc.scalar.copy(out=qTs[qi][0:width, 0:n, :], in_=ptr[pi][0:width, 0:n, :]),
                       reads=[ptr_b[pi]], writes=[qTs_b[qi]])
                    c0 = chunk0 + done
                    dma("sp", self.qk_d[s, c0:c0 + n, 0:width, t * 128:(t + 1) * 128].rearrange("n p t -> p n t"),
                        qTs[qi][0:width, 0:n, :], qTs_b[qi], reads=[qTs_b[qi]])
                    done += n

            segs = []
            segs.append(("qk", 0, 8, "a_q", 0, 0, 0))
            segs.append(("qk", 512, 8, "a_k", 0, 4, 1))
            segs.append(("v", 1024, 512, 0))
            for g in range(3):
                base = 1536 + g * 768
                segs.append(("qk", base, 4, "b_q", g * 64, 8 + 4 * g, 2 + 2 * g))
                segs.append(("qk", base + 256, 4, "b_k", g * 64, 10 + 4 * g, 3 + 2 * g))
                segs.append(("v", base + 512, 256, 512 + g * 256))
            segs.append(("cq", 3840))
            segs.append(("ckv", 4096))
            segs.append(("qk", 4256, 8, "d_q", 0, 20, 8))
            segs.append(("qk", 4768, 2, "d_k", 0, 24, 9))
            segs.append(("v", 4896, 128, 1792))
            bounds = [0, 512, 1024, 1536, 2048, 2560, 3072, 3584, 4096, 4256, 4768, 5024]
            qn_off = {}
            o = 0
            for sg in segs:
                if sg[0] == "qk":
                    qn_off[sg[6]] = o
                    o += sg[2] * 64
            QC_OFF = o
            KC_OFF = o + 768

            def prologue(gi):
                s, t = divmod(gi, NT)
                sl = gi % 2
                op("act", lambda: nc.scalar.activation(out=junk[:], in_=xt[sl][:], func=AF.Square, accum_out=st1[sl][:, 0:1]),
                   reads=[xt_b[sl]], writes=[junk_b, st1_b[sl]])
                op("act", lambda: nc.scalar.activation(out=st1[sl][:, 1:2], in_=st1[sl][:, 0:1], func=AF.Sqrt, bias=EPS, scale=1.0 / D),
                   reads=[st1_b[sl]], writes=[st1_b[sl]])
                op("dve", lambda: nc.vector.reciprocal(out=st1[sl][:, 2:3], in_=st1[sl][:, 1:2]), reads=[st1_b[sl]], writes=[st1_b[sl]])
                if VARIANT & 1:
                    op("dve", lambda: nc.vector.scalar_tensor_tensor(out=hb[sl][:], in0=xt[sl][:], scalar=st1[sl][:, 2:3], in1=G("g1", 0, D),
                                                                     op0=ALU.mult, op1=ALU.mult),
                       reads=[xt_b[sl], st1_b[sl], gains_b], writes=[hb_b[sl]])
                else:
                    op("act", lambda: nc.scalar.activation(out=hx[:], in_=xt[sl][:], func=AF.Copy, scale=st1[sl][:, 2:3]),
                       reads=[xt_b[sl], st1_b[sl]], writes=[hx_b])
                    op("pool", lambda: nc.gpsimd.tensor_tensor(out=hb[sl][:], in0=hx[:], in1=G("g1", 0, D), op=ALU.mult),
                       reads=[hx_b, gains_b], writes=[hb_b[sl]])
                pi = cnt["tr"] % 2; cnt["tr"] += 1
                for kc in range(8):
                    op("pe", lambda kc=kc: nc.tensor.transpose(ptr[pi][:, kc, :], hb[sl][:, kc * 128:(kc + 1) * 128], ident[:]),
                       reads=[hb_b[sl], const_b], writes=[ptr_b[pi]], inc=(kc == 7))
                op("act", lambda: nc.scalar.copy(out=hT[sl][:], in_=ptr[pi][:]), reads=[ptr_b[pi]], writes=[hT_b[sl]])
                dma("sp", self.hT_d[s, :, :, t * 128:(t + 1) * 128].rearrange("k p t -> p k t"), hT[sl][:], hT_b[sl], reads=[hT_b[sl]])

            def load_x(gi):
                s, t = divmod(gi, NT)
                dma("sp", xt[gi % 2][:], xin[s, t * 128:(t + 1) * 128, :], xt_b[gi % 2], writes=[xt_b[gi % 2]])

            def proc_seg(gi, sg, pjt, pjb, c0):
                s, t = divmod(gi, NT)
                sl = gi % 2
                q3 = gi % 2
                kind = sg[0]
                a = sg[1] - c0
                if kind == "v":
                    w, voff = sg[2], sg[3]
                    op("act", lambda: nc.scalar.copy(out=vst[sl][:, voff:voff + w], in_=pjt[:, a:a + w]), reads=[pjb], writes=[vst_b[sl]])
                elif kind == "qk":
                    _, col, H, gname, goff_, chunk0, slot = sg
                    Wd = H * 64
                    src3 = pjt[:, a:a + Wd].rearrange("p (h d) -> p h d", h=H)
                    qo = qn_off[slot]
                    d3 = qn[q3][:, qo:qo + Wd].rearrange("p (h d) -> p h d", h=H)
                    qb = qn_b[q3][slot]
                    submit(src3, [pjb], H, 64, G(gname, goff_, 64), (0, 32, cs32[:, t, 0:32], cs32[:, t, 32:64]), d3, [qb],
                           after=lambda: deferred.append((gi + 1, lambda: transpose_out(s, t, qn[q3], qb, qo, 128, Wd // 128, chunk0))))
                elif kind == "cq":
                    src3 = pjt[:, a:a + 256].rearrange("p (h d) -> p h d", h=1)
                    d3 = cqn[sl][:, 0:256].rearrange("p (h d) -> p h d", h=1)
                    submit(src3, [pjb], 1, 256, G("c_qa", 0, 256), None, d3, [cqn_b[sl]])
                elif kind == "ckv":
                    src3 = pjt[:, a:a + 128].rearrange("p (h d) -> p h d", h=1)
                    d3 = cqn[sl][:, 256:384].rearrange("p (h d) -> p h d", h=1)
                    op("act", lambda: nc.scalar.copy(out=krp[sl][:], in_=pjt[:, a + 128:a + 160]), reads=[pjb], writes=[krp_b[sl]])
                    submit(src3, [pjb], 1, 128, G("c_kva", 0, 128), None, d3, [cqn_b[sl]],
                           after=lambda: deferred.append((gi + 1, lambda: c_stage2(gi))))

            def c_stage2(gi):
                s, t = divmod(gi, NT)
                sl = gi % 2
                q3 = gi % 2
                pi = cnt["tr"] % 2; cnt["tr"] += 1
                for j in range(3):
                    op("pe", lambda j=j: nc.tensor.transpose(ptr[pi][:, j, :], cqn[sl][:, j * 128:(j + 1) * 128], ident[:]),
                       reads=[cqn_b[sl], const_b], writes=[ptr_b[pi]], inc=(j == 2))
                op("act", lambda: nc.scalar.copy(out=cT[sl][:], in_=ptr[pi][:, 0:3, :]), reads=[ptr_b[pi]], writes=[cT_b[sl]])
                for (c0, c1) in ((0, 512), (512, 768)):
                    for kc in range(2):
                        op("pe", lambda kc=kc, c0=c0, c1=c1: nc.tensor.matmul(pq[:, c0:c1], lhsT=cT[sl][:, kc, :], rhs=w_uq[:, kc, c0:c1],
                                                                              start=(kc == 0), stop=(kc == 1)),
                           reads=[cT_b[sl], w_c_b], writes=[pq_b], inc=(kc == 1))
                src3 = pq[:, 0:768].rearrange("p (h d) -> p h d", h=8)
                d3 = qn[q3][:, QC_OFF:QC_OFF + 768].rearrange("p (h d) -> p h d", h=8)
                qb = qn_b[q3][10]
                submit(src3, [pq_b], 8, 96, G("c_q", 0, 96), (64, 16, cs16[:, t, 0:16], cs16[:, t, 16:32]), d3, [qb],
                       after=lambda: deferred.append((gi + 2, lambda: transpose_out(s, t, qn[q3], qb, QC_OFF, 96, 8, 25))))
                for (c0, c1) in ((0, 512), (512, 1024)):
                    op("pe", lambda c0=c0, c1=c1: nc.tensor.matmul(pkv[:, c0:c1], lhsT=cT[sl][:, 2, :], rhs=w_ukv[:, c0:c1], start=True, stop=True),
                       reads=[cT_b[sl], w_c_b], writes=[pkv_b])
                kv3 = pkv[:].rearrange("p (h d) -> p h d", h=8)
                op("act", lambda: nc.scalar.copy(out=vst[sl][:, 1280:1792].rearrange("p (h d) -> p h d", h=8), in_=kv3[:, :, 64:128]),
                   reads=[pkv_b], writes=[vst_b[sl]])
                op("act", lambda: nc.scalar.copy(out=kcb[sl][:, :, 0:64], in_=kv3[:, :, 0:64]), reads=[pkv_b], writes=[kcb_b[sl]])
                op("dve", lambda: nc.vector.tensor_copy(out=kcb[sl][:, :, 64:96], in_=krp[sl][:].unsqueeze(1).to_broadcast([128, 8, 32])),
                   reads=[krp_b[sl]], writes=[kcb_b[sl]])
                d3k = qn[q3][:, KC_OFF:KC_OFF + 768].rearrange("p (h d) -> p h d", h=8)
                kb_ = qn_b[q3][11]
                submit(kcb[sl][:], [kcb_b[sl]], 8, 96, G("c_k", 0, 96), (64, 16, cs16[:, t, 0:16], cs16[:, t, 16:32]), d3k, [kb_],
                       after=lambda: deferred.append((gi + 2, lambda: transpose_out(s, t, qn[q3], kb_, KC_OFF, 96, 8, 33))))
                dma("sp", self.v_d[s, t * 128:(t + 1) * 128, :], vst[sl][:], vst_b[sl], reads=[vst_b[sl]])

            step = {"n": 0}

            class _Lag(deque):
                def append(self, item):
                    deque.append(self, (step["n"] + 3, item[1]))

            deferred = _Lag()

            def run_deferred(gi, limit=2):
                n = 0
                while deferred and n < limit and deferred[0][0] <= step["n"]:
                    _, fn = deferred.popleft()
                    fn()
                    n += 1

            NTILES = NSEQ * NT
            load_x(0)
            prologue(0)
            if NTILES > 1:
                load_x(1)
            ci_glob = 0
            for gi in range(NTILES):
                sl = gi % 2
                for ci in range(len(bounds) - 1):
                    c0, c1 = bounds[ci], bounds[ci + 1]
                    pi = ci_glob % NPJ; ci_glob += 1
                    for kc in range(8):
                        op("pe", lambda kc=kc, pi=pi, c0=c0, c1=c1: nc.tensor.matmul(pj[pi][:, 0:c1 - c0], lhsT=hT[sl][:, kc, :], rhs=w_in[:, kc, c0:c1],
                                                                                     start=(kc == 0), stop=(kc == 7)),
                           reads=[hT_b[sl], w_in_b], writes=[pj_b[pi]], inc=(kc == 7))
                    for sg in segs:
                        if c0 <= sg[1] < c1:
                            proc_seg(gi, sg, pj[pi], pj_b[pi], c0)
                    step["n"] += 1
                    run_deferred(gi)
                    if ci == 4 and gi + 1 < NTILES:
                        prologue(gi + 1)
                        if gi + 2 < NTILES:
                            load_x(gi + 2)
            while deferred or jobs:
                if jobs:
                    advance()
                step["n"] += 1
                run_deferred(0, limit=100)

    def attn_common(self, es):
        nc = self.nc

        def sb(name, shape, dt):
            return es.enter_context(nc.sbuf_tensor(self.nm(name), shape, dt))

        def ps(name, shape, dt):
            return es.enter_context(nc.psum_tensor(self.nm(name), shape, dt))
        return sb, ps

    def phase_A(self, l):
        nc, tr, W = self.nc, self.tr, self.W
        S, NT, NSEQ, NQT = self.S, self.NT, self.NSEQ, self.NQT
        op, dma = tr.op, tr.dma
        li = lambda_init(l)
        with ExitStack() as es:
            sb, ps = self.attn_common(es)
            ones = sb("ones", [128, 128], BF16); ones_b = Buf("ones")
            op("dve", lambda: nc.vector.memset(ones[:], 1.0), writes=[ones_b])
            lamt = sb("lamt", [128, 256], F32); sc_b = Buf("sc")
            prd = sb("prd", [128, 128], F32)
            sm = sb("sm", [128, 8], F32)
            gsub = sb("gsub", [128, 2], F32)
            dma("sp", lamt[:], W["a_lambda"][l].rearrange("a d -> (a d)").partition_broadcast(128), sc_b, writes=[sc_b])
            dma("sp", gsub[:, 0:1], W["a_subln_g"][l].rearrange("(p o) -> p o", o=1), sc_b, writes=[sc_b])
            op("dve", lambda: nc.vector.tensor_tensor(out=prd[:].rearrange("p (a d) -> p a d", a=2),
                                                      in0=lamt[:].rearrange("p (a b d) -> p a b d", a=2, b=2)[:, :, 0, :],
                                                      in1=lamt[:].rearrange("p (a b d) -> p a b d", a=2, b=2)[:, :, 1, :], op=ALU.mult),
               reads=[sc_b], writes=[sc_b])
            op("dve", lambda: nc.vector.tensor_reduce(out=sm[:, 0:2], in_=prd[:].rearrange("p (a d) -> p a d", a=2), axis=AX.X, op=ALU.add),
               reads=[sc_b], writes=[sc_b])
            op("act", lambda: nc.scalar.activation(out=sm[:, 2:4], in_=sm[:, 0:2], func=AF.Exp), reads=[sc_b], writes=[sc_b])
            op("dve", lambda: nc.vector.tensor_tensor(out=sm[:, 4:5], in0=sm[:, 3:4], in1=sm[:, 2:3], op=ALU.subtract), reads=[sc_b], writes=[sc_b])
            op("dve", lambda: nc.vector.tensor_scalar(out=sm[:, 5:6], in0=sm[:, 4:5], scalar1=-li, scalar2=None, op0=ALU.add), reads=[sc_b], writes=[sc_b])
            op("dve", lambda: nc.vector.tensor_scalar(out=gsub[:, 1:2], in0=gsub[:, 0:1], scalar1=1.0 - li, scalar2=None, op0=ALU.mult),
               reads=[sc_b], writes=[sc_b])
            nlam = sm[:, 5:6]
            gsc = gsub[:, 1:2]

            QT = [sb("QT", [128, S], BF16) for _ in range(2)]
            KT = [sb("KT", [128, S], BF16) for _ in range(2)]
            VA = [sb("VA", [128, NT, 128], BF16) for _ in range(2)]
            slot_b = [Buf("slot") for _ in range(2)]
            NE = 8
            E = [sb("E", [128, 512], BF16) for _ in range(NE)]; E_b = [Buf("E") for _ in range(NE)]
            acc = [[[sb("acc", [128, 512], F32) for _ in range(2)] for _ in range(2)] for _ in range(2)]
            acc_b = [[[Buf("acc") for _ in range(2)] for _ in range(2)] for _ in range(2)]
            accb = [[[sb("accb", [128, 512], BF16) for _ in range(2)] for _ in range(2)] for _ in range(2)]
            accb_b = [[[Buf("accb") for _ in range(2)] for _ in range(2)] for _ in range(2)]
            ev = [sb("ev", [128, 512], F32) for _ in range(2)]; ev_b = [Buf("ev") for _ in range(2)]
            rr = [sb("rr", [128, 512], F32) for _ in range(3)]; rr_b = [Buf("rr") for _ in range(3)]
            o32 = [sb("o32", [128, S], F32) for _ in range(2)]; o32_b = [[Buf("o32") for _ in range(NQT)] for _ in range(2)]
            osq = sb("osq", [128, 512], BF16); osq_b = Buf("osq")
            rms = sb("rms", [128, 512], F32); rms_b = Buf("rms")
            ob = [sb("ob", [128, 512], BF16) for _ in range(2)]; ob_b = [Buf("ob") for _ in range(2)]
            pS = [ps("pS", [128, 512], F32) for _ in range(4)]; pS_b = [PBuf("pS") for _ in range(4)]
            pU = [ps("pU", [128, 512], F32) for _ in range(3)]
            pU_b = [PBuf("pU") for _ in range(3)]
            evs = [sb("evs", [128, 512], F32) for _ in range(3)]; evs_b = [Buf("evs") for _ in range(3)]

            units = [(s, h) for s in range(NSEQ) for h in range(4)]

            def load(ui):
                s, h = units[ui]
                sl = ui % 2
                dma("sp", QT[sl][:], self.qk_d[s, h], slot_b[sl], writes=[slot_b[sl]])
                dma("sp", KT[sl][:], self.qk_d[s, 4 + h], slot_b[sl], writes=[slot_b[sl]])
                dma("sp", VA[sl][:], self.v_d[s].rearrange("(n p) f -> p n f", p=128)[:, :, h * 128:(h + 1) * 128], slot_b[sl], writes=[slot_b[sl]])

            def subln(ui):
                s, h = units[ui]
                u2 = ui % 2
                for qt in range(NQT):
                    q0 = qt * 512
                    osl = o32[u2][:, q0:q0 + 512]
                    ob_ = o32_b[u2][qt]
                    op("pool", lambda osl=osl: nc.gpsimd.tensor_tensor(out=osq[:], in0=osl, in1=osl, op=ALU.mult), reads=[ob_], writes=[osq_b])
                    si = cs["s"] % 4; cs["s"] += 1
                    op("pe", lambda si=si: nc.tensor.matmul(pS[si][:], lhsT=ones[:], rhs=osq[:], start=True, stop=True), reads=[osq_b, ones_b], writes=[pS_b[si]])
                    op("act", lambda si=si: nc.scalar.activation(out=rms[:], in_=pS[si][:], func=AF.Sqrt, bias=EPS, scale=1.0 / 128), reads=[pS_b[si]], writes=[rms_b])
                    op("dve", lambda: nc.vector.reciprocal(out=rr[2][:], in_=rms[:]), reads=[rms_b], writes=[rr_b[2]])
                    oi = cs["o"] % 2; cs["o"] += 1
                    op("dve", lambda oi=oi, osl=osl: nc.vector.scalar_tensor_tensor(out=ob[oi][:], in0=osl, scalar=gsc, in1=rr[2][:], op0=ALU.mult, op1=ALU.mult),
                       reads=[ob_, rr_b[2], sc_b], writes=[ob_b[oi]])
                    dma("sp", self.oT_d[s, h, :, q0:q0 + 512], ob[oi][:], ob_b[oi], reads=[ob_b[oi]])

            load(0)
            cs = {"s": 0, "e": 0, "o": 0, "q": 0}
            LA = 3
            for ui in range(len(units)):
                s, h = units[ui]
                sl = ui % 2
                u2 = ui % 2
                if ui + 1 < len(units):
                    load(ui + 1)
                for qt in range(NQT):
                    q0 = qt * 512
                    a2 = cs["q"] % 2; cs["q"] += 1
                    ids = []
                    for i in range(NT + LA):
                        if i < NT:
                            rec = []
                            for c in range(2):
                                si = cs["s"] % 4; cs["s"] += 1
                                ei = cs["e"] % NE; cs["e"] += 1
                                p0 = c * 64
                                op("pe", lambda si=si, p0=p0, i=i: nc.tensor.matmul(pS[si][:], lhsT=KT[sl][p0:p0 + 64, i * 128:(i + 1) * 128],
                                                                                   rhs=QT[sl][p0:p0 + 64, q0:q0 + 512], start=True, stop=True),
                                   reads=[slot_b[sl]], writes=[pS_b[si]])
                                op("act", lambda si=si, ei=ei: nc.scalar.activation(out=E[ei][:], in_=pS[si][:], func=AF.Exp, scale=0.125),
                                   reads=[pS_b[si]], writes=[E_b[ei]])
                                rec.append(ei)
                                if c == 1:
                                    continue
                                if i % 4 == 3:
                                    me, eng, ak = "dve", nc.vector, (1, 0)
                                elif i % 4 == 1:
                                    me, eng, ak = "pool", nc.gpsimd, (0, 1)
                                else:
                                    me, eng, ak = "pool", nc.gpsimd, (0, 0)
                                A_ = acc[a2][ak[0]][ak[1]]; Ab = acc_b[a2][ak[0]][ak[1]]
                                mine = [x for x in range(NT) if (x % 4 == 3) == (i % 4 == 3) and (x % 4 == 1) == (i % 4 == 1)]
                                if i == mine[0]:
                                    op(me, lambda eng=eng, A_=A_, ei=ei: eng.tensor_copy(out=A_[:], in_=E[ei][:]), reads=[E_b[ei]], writes=[Ab])
                                elif i != mine[-1]:
                                    op(me, lambda eng=eng, A_=A_, ei=ei: eng.tensor_tensor(out=A_[:], in0=A_[:], in1=E[ei][:], op=ALU.add),
                                       reads=[E_b[ei], Ab], writes=[Ab])
                                else:
                                    op(me, lambda eng=eng, A_=A_, ei=ei, ak=ak: eng.tensor_tensor(out=accb[a2][ak[0]][ak[1]][:], in0=A_[:], in1=E[ei][:], op=ALU.add),
                                       reads=[E_b[ei], Ab], writes=[accb_b[a2][ak[0]][ak[1]]])
                            ids.append(rec)
                        if i >= LA:
                            j = i - LA
                            for c in range(2):
                                ei = ids[j][c]
                                op("pe", lambda c=c, ei=ei, j=j: nc.tensor.matmul(pU[c][:], lhsT=VA[sl][:, j, :], rhs=E[ei][:], start=(j == 0), stop=(j == NT - 1)),
                                   reads=[E_b[ei], slot_b[sl]], writes=[pU_b[c]], inc=(c == 0))
                                if c == 1:
                                    op("pe", lambda ei=ei, j=j: nc.tensor.matmul(pU[2][:], lhsT=ones[:], rhs=E[ei][:], start=(j == 0), stop=(j == NT - 1)),
                                       reads=[E_b[ei], ones_b], writes=[pU_b[2]])
                    op("dve", lambda: nc.vector.tensor_copy(out=evs[0][:], in_=pU[0][:]), reads=[pU_b[0]], writes=[evs_b[0]])
                    op("act", lambda: nc.scalar.copy(out=evs[1][:], in_=pU[1][:]), reads=[pU_b[1]], writes=[evs_b[1]])
                    op("dve", lambda: nc.vector.tensor_copy(out=evs[2][:], in_=pU[2][:]), reads=[pU_b[2]], writes=[evs_b[2]])
                    si = cs["s"] % 4; cs["s"] += 1
                    for n_, ak in enumerate(((0, 0), (0, 1), (1, 0))):
                        op("pe", lambda si=si, ak=ak, n_=n_: nc.tensor.matmul(pS[si][:], lhsT=ones[:], rhs=accb[a2][ak[0]][ak[1]][:], start=(n_ == 0), stop=(n_ == 2)),
                           reads=[accb_b[a2][ak[0]][ak[1]], ones_b], writes=[pS_b[si]], inc=(n_ == 2))
                    op("dve", lambda si=si: nc.vector.reciprocal(out=rr[0][:], in_=pS[si][:]), reads=[pS_b[si]], writes=[rr_b[0]])
                    op("dve", lambda: nc.vector.reciprocal(out=rr[1][:], in_=evs[2][:]), reads=[evs_b[2]], writes=[rr_b[1]])
                    for c in range(2):
                        op("dve", lambda c=c: nc.vector.tensor_tensor(out=ev[c][:], in0=evs[c][:], in1=rr[c][:], op=ALU.mult),
                           reads=[evs_b[c], rr_b[c]], writes=[ev_b[c]])
                    op("dve", lambda: nc.vector.scalar_tensor_tensor(out=o32[u2][:, q0:q0 + 512], in0=ev[1][:], scalar=nlam, in1=ev[0][:], op0=ALU.mult, op1=ALU.add),
                       reads=[ev_b[0], ev_b[1], sc_b], writes=[o32_b[u2][qt]])
                    if qt == 0 and ui > 0:
                        subln(ui - 1)
            subln(len(units) - 1)

    def phase_64(self, l, mixer):
        nc, tr, W = self.nc, self.tr, self.W
        S, NT, NSEQ, NQT = self.S, self.NT, self.NSEQ, self.NQT
        op, dma = tr.op, tr.dma
        with ExitStack() as es:
            sb, ps = self.attn_common(es)
            NE = 12
            NPS = 6
            E = [sb("E", [128, 512], BF16) for _ in range(NE)]; E_b = [Buf("E") for _ in range(NE)]
            R = [sb("R", [128, 512], F32) for _ in range(2)]; R_b = [Buf("R") for _ in range(2)]
            ob = [sb("ob", [128, 512], BF16) for _ in range(2)]; ob_b = [Buf("ob") for _ in range(2)]
            pS = [ps("pS", [128, 512], F32) for _ in range(NPS)]; pS_b = [PBuf("pS") for _ in range(NPS)]
            pT = [ps("pT", [128, 512], F32) for _ in range(2)]; pT_b = [PBuf("pT") for _ in range(2)]
            cs = {"s": 0, "e": 0, "t": 0, "m": 0}
            const_b = Buf("const")
            if mixer in ("b", "d"):
                masks = sb("masks", [128, MASK_W], BF16)
                dma("sp", masks[:], self.masks_d, const_b, writes=[const_b])
            if mixer == "d":
                es8 = sb("es8", [128, 16], F32)
                dma("sp", es8[:, 0:8], W["d_sink"][l].partition_broadcast(128), const_b, writes=[const_b])
                op("act", lambda: nc.scalar.activation(out=es8[:, 8:16], in_=es8[:, 0:8], func=AF.Exp), reads=[const_b], writes=[const_b])

            def band(mk, qt):
                O, Wd, R_, dil = MASK_DEF[mk]
                q0 = qt * 512
                res = []
                for kb in range(NT):
                    dl = kb * 128 - q0
                    if -R_ - 127 <= dl <= R_ + 511:
                        j0 = MASK_COL[mk] + O - dl
                        res.append((kb, j0))
                return res

            pipe = deque()
            LA = 5

            def pipe_flush(keep):
                while len(pipe) > keep:
                    pipe.popleft()()

            def run_head(terms, parity, scale, sink_ap, out_ap, tail_fn=None):
                ti = cs["t"] % 2; cs["t"] += 1
                n = len(terms)
                for i in range(n):
                    kT, qT, va, mk, bufs = terms[i]
                    si = cs["s"] % NPS; cs["s"] += 1
                    ei = cs["e"] % NE; cs["e"] += 1
                    op("pe", lambda si=si, kT=kT, qT=qT: nc.tensor.matmul(pS[si][:], lhsT=kT, rhs=qT, start=True, stop=True),
                       reads=bufs, writes=[pS_b[si]])
                    op("act", lambda si=si, ei=ei: nc.scalar.activation(out=E[ei][:], in_=pS[si][:], func=AF.Exp, scale=scale),
                       reads=[pS_b[si]], writes=[E_b[ei]])
                    if mk is not None:
                        me = "pool" if cs["m"] % 4 == 0 else "dve"
                        cs["m"] += 1
                        eng = nc.gpsimd if me == "pool" else nc.vector
                        op(me, lambda eng=eng, ei=ei, mk=mk: eng.tensor_tensor(out=E[ei][:], in0=E[ei][:], in1=mk, op=ALU.mult),
                           reads=[E_b[ei], const_b], writes=[E_b[ei]])

                    def back(i=i, ei=ei, va=va, bufs=bufs):
                        op("pe", lambda: nc.tensor.matmul(pT[ti][:], lhsT=va, rhs=E[ei][:], start=(i == 0), stop=(i == n - 1)),
                           reads=[E_b[ei]] + list(bufs), writes=[pT_b[ti]])
                        if i == n - 1:
                            if tail_fn is not None:
                                tail_fn(ti)
                            else:
                                tail(ti, parity, sink_ap, out_ap)
                    pipe.append(back)
                    pipe_flush(LA)

            def tail(ti, parity, sink_ap, out_ap):
                ur = (0, 64) if parity == 0 else (64, 128)
                lr = (64, 128) if parity == 0 else (0, 64)
                T = pT[ti]
                Rt = R[ti]
                if sink_ap is not None:
                    op("dve", lambda: nc.vector.tensor_scalar(out=Rt[lr[0]:lr[1], :], in0=T[lr[0]:lr[1], :], scalar1=sink_ap(lr), scalar2=None, op0=ALU.add),
                       reads=[pT_b[ti], const_b], writes=[R_b[ti]])
                    op("dve", lambda: nc.vector.reciprocal(out=Rt[lr[0]:lr[1], :], in_=Rt[lr[0]:lr[1], :]), reads=[R_b[ti]], writes=[R_b[ti]])
                else:
                    op("dve", lambda: nc.vector.reciprocal(out=Rt[lr[0]:lr[1], :], in_=T[lr[0]:lr[1], :]), reads=[pT_b[ti]], writes=[R_b[ti]])
                op("dve", lambda: nc.vector.tensor_tensor(out=ob[ti][ur[0]:ur[1], :], in0=T[ur[0]:ur[1], :], in1=Rt[lr[0]:lr[1], :], op=ALU.mult),
                   reads=[pT_b[ti], R_b[ti]], writes=[ob_b[ti]])
                dma("sp", out_ap(ur), ob[ti][ur[0]:ur[1], :], ob_b[ti], reads=[ob_b[ti]])

            def fill_ones(va_tile, buf, nhead_slots):
                for p in range(nhead_slots):
                    c0 = 64 if p % 2 == 0 else 0
                    op("pool", lambda p=p, c0=c0: nc.gpsimd.memset(va_tile[:, :, p, c0:c0 + 64], 1.0), writes=[buf])

            if mixer == "c":
                scale = 96 ** -0.5
                QT = [sb("QT", [128, S], BF16) for _ in range(2)]
                KT = [sb("KT", [128, S], BF16) for _ in range(2)]
                VA = [sb("VA", [128, NT, 2, 128], BF16) for _ in range(2)]
                slot_b = [Buf("slot") for _ in range(2)]
                for i in range(2):
                    fill_ones(VA[i], slot_b[i], 2)
                units = [(s, h) for s in range(NSEQ) for h in range(8)]

                def load(ui):
                    s, h = units[ui]
                    sl = ui % 2
                    p = h % 2
                    dma("sp", QT[sl][0:96, :], self.qk_d[s, 25 + h, 0:96, :], slot_b[sl], writes=[slot_b[sl]])
                    dma("sp", KT[sl][0:96, :], self.qk_d[s, 33 + h, 0:96, :], slot_b[sl], writes=[slot_b[sl]])
                    vc = 1280 + h * 64
                    c0 = 0 if p == 0 else 64
                    dma("sp", VA[sl][:, :, p, c0:c0 + 64], self.v_d[s].rearrange("(n p) f -> p n f", p=128)[:, :, vc:vc + 64], slot_b[sl], writes=[slot_b[sl]])

                load(0)
                for ui in range(len(units)):
                    s, h = units[ui]
                    sl = ui % 2
                    p = h % 2
                    if ui + 1 < len(units):
                        pipe_flush(0)
                        load(ui + 1)
                    for qt in range(NQT):
                        q0 = qt * 512
                        terms = [(KT[sl][0:96, kb * 128:(kb + 1) * 128], QT[sl][0:96, q0:q0 + 512], VA[sl][:, kb, p, :], None, [slot_b[sl]])
                                 for kb in range(NT)]
                        run_head(terms, p, scale, None, lambda ur, s=s, h=h, q0=q0: self.oT_d[s, 6 + h // 2, ur[0]:ur[1], q0:q0 + 512])
            elif mixer == "d":
                scale = 0.125
                QT = [sb("QT", [128, S], BF16) for _ in range(2)]
                KT = [sb("KT", [128, S], BF16) for _ in range(2)]
                VA = [sb("VA", [128, NT, 2, 128], BF16) for _ in range(2)]
                slot_b = [Buf("slot") for _ in range(2)]
                for i in range(2):
                    fill_ones(VA[i], slot_b[i], 2)
                units = [(s, pr) for s in range(NSEQ) for pr in range(4)]

                def load(ui):
                    s, pr = units[ui]
                    sl = ui % 2
                    kvh = pr // 2
                    dma("sp", QT[sl][:], self.qk_d[s, 20 + pr], slot_b[sl], writes=[slot_b[sl]])
                    for hh in range(2):
                        dma("sp", KT[sl][hh * 64:hh * 64 + 64, :], self.qk_d[s, 24, kvh * 64:kvh * 64 + 64, :], slot_b[sl], writes=[slot_b[sl]])
                    vc = 1792 + kvh * 64
                    vsrc = self.v_d[s].rearrange("(n p) f -> p n f", p=128)[:, :, vc:vc + 64]
                    dma("sp", VA[sl][:, :, 0, 0:64], vsrc, slot_b[sl], writes=[slot_b[sl]])
                    dma("sp", VA[sl][:, :, 1, 64:128], vsrc, slot_b[sl], writes=[slot_b[sl]])

                load(0)
                for ui in range(len(units)):
                    s, pr = units[ui]
                    sl = ui % 2
                    kvh = pr // 2
                    if ui + 1 < len(units):
                        pipe_flush(0)
                        load(ui + 1)
                    for qt in range(NQT):
                        q0 = qt * 512
                        bd = band("d", qt)
                        for p in range(2):
                            h = pr * 2 + p
                            terms = [(KT[sl][p * 64:p * 64 + 64, kb * 128:(kb + 1) * 128], QT[sl][p * 64:p * 64 + 64, q0:q0 + 512],
                                      VA[sl][:, kb, p, :], masks[:, j0:j0 + 512], [slot_b[sl]]) for kb, j0 in bd]
                            run_head(terms, p, scale, lambda lr, h=h: es8[lr[0]:lr[1], 8 + h:9 + h],
                                     lambda ur, s=s, pr=pr, q0=q0: self.oT_d[s, 10 + pr, ur[0]:ur[1], q0:q0 + 512])
            else:
                scale = 0.125
                QT = [sb("QT", [128, S], BF16) for _ in range(2)]
                KT = [sb("KT", [128, S], BF16) for _ in range(2)]
                VA = [sb("VA", [128, NT, 2, 128], BF16) for _ in range(2)]
                slot_b = [Buf("slot") for _ in range(2)]
                for i in range(2):
                    fill_ones(VA[i], slot_b[i], 2)
                bacc = [sb("bacc", [128, S], F32) for _ in range(2)]
                bacc_b = [[Buf("bacc") for _ in range(NQT)] for _ in range(2)]
                subunits = [(s, pr, g) for s in range(NSEQ) for pr in range(2) for g in range(3)]

                def load(k):
                    s, pr, g = subunits[k]
                    sl = k % 2
                    dma("sp", QT[sl][:], self.qk_d[s, 8 + 4 * g + pr], slot_b[sl], writes=[slot_b[sl]])
                    dma("sp", KT[sl][:], self.qk_d[s, 10 + 4 * g + pr], slot_b[sl], writes=[slot_b[sl]])
                    for p in range(2):
                        vc = 512 + g * 256 + (pr * 2 + p) * 64
                        c0 = 0 if p == 0 else 64
                        dma("sp", VA[sl][:, :, p, c0:c0 + 64], self.v_d[s].rearrange("(n p) f -> p n f", p=128)[:, :, vc:vc + 64],
                            slot_b[sl], writes=[slot_b[sl]])

                def tail_b(ti, p, g, qt, s, pr):
                    q0 = qt * 512
                    A_ = bacc[p][:, q0:q0 + 512]
                    Ab = bacc_b[p][qt]
                    if g == 0:
                        op("dve", lambda: nc.vector.tensor_copy(out=A_, in_=pT[ti][:]), reads=[pT_b[ti]], writes=[Ab])
                        return
                    if g == 1:
                        op("dve", lambda: nc.vector.tensor_tensor(out=A_, in0=pT[ti][:], in1=A_, op=ALU.add), reads=[pT_b[ti], Ab], writes=[Ab])
                        return
                    ur = (0, 64) if p == 0 else (64, 128)
                    lr = (64, 128) if p == 0 else (0, 64)
                    T = pT[ti]
                    op("dve", lambda: nc.vector.tensor_tensor(out=R[ti][lr[0]:lr[1], :], in0=T[lr[0]:lr[1], :], in1=bacc[p][lr[0]:lr[1], q0:q0 + 512], op=ALU.add),
                       reads=[pT_b[ti], Ab], writes=[R_b[ti]])
                    op("dve", lambda: nc.vector.reciprocal(out=R[ti][lr[0]:lr[1], :], in_=R[ti][lr[0]:lr[1], :]), reads=[R_b[ti]], writes=[R_b[ti]])
                    op("dve", lambda: nc.vector.tensor_tensor(out=T[ur[0]:ur[1], :], in0=T[ur[0]:ur[1], :], in1=bacc[p][ur[0]:ur[1], q0:q0 + 512], op=ALU.add),
                       reads=[pT_b[ti], Ab], writes=[pT_b[ti]])
                    op("dve", lambda: nc.vector.tensor_tensor(out=ob[ti][ur[0]:ur[1], :], in0=T[ur[0]:ur[1], :], in1=R[ti][lr[0]:lr[1], :], op=ALU.mult),
                       reads=[pT_b[ti], R_b[ti]], writes=[ob_b[ti]])
                    dma("sp", self.oT_d[s, 4 + pr, ur[0]:ur[1], q0:q0 + 512], ob[ti][ur[0]:ur[1], :], ob_b[ti], reads=[ob_b[ti]])

                load(0)
                for k in range(len(subunits)):
                    s, pr, g = subunits[k]
                    sl = k % 2
                    if k + 1 < len(subunits):
                        pipe_flush(0)
                        load(k + 1)
                    for qt in range(NQT):
                        q0 = qt * 512
                        for p in range(2):
                            terms = [(KT[sl][p * 64:p * 64 + 64, kb * 128:(kb + 1) * 128], QT[sl][p * 64:p * 64 + 64, q0:q0 + 512],
                                      VA[sl][:, kb, p, :], masks[:, j0:j0 + 512], [slot_b[sl]]) for kb, j0 in band("b%d" % g, qt)]
                            run_head(terms, p, scale, None, None,
                                     tail_fn=lambda ti, p=p, g=g, qt=qt, s=s, pr=pr: tail_b(ti, p, g, qt, s, pr))
            pipe_flush(0)

    def phase_M1g(self, l, pre=None):
        nc, tr, W = self.nc, self.tr, self.W
        S, NSEQ, NQT = self.S, self.NSEQ, self.NQT
        op, dma = tr.op, tr.dma
        with ExitStack() as es:
            sb, ps = self.attn_common(es)
            wg, wg_b, wbr, wbr_b, br_chunks = pre if pre is not None else self.load_m1g_weights(es, l)
            hTt = [sb("hTt", [128, 8, 512], BF16) for _ in range(2)]; hT_b = [Buf("hTt") for _ in range(2)]
            oTt = [sb("oTt", [128, NOC, 512], BF16) for _ in range(2)]; oT_b = [Buf("oTt") for _ in range(2)]
            sg = [sb("sg", [128, 512], F32) for _ in range(4)]; sg_b = [Buf("sg") for _ in range(4)]
            pr_ = [sb("pr", [128, 512], F32) for _ in range(4)]; pr_b = [Buf("pr") for _ in range(4)]
            s2 = [sb("s2", [128, 512], F32) for _ in range(2)]; s2_b = [Buf("s2") for _ in range(2)]
            mg = [sb("mg", [128, 8, 512], BF16) for _ in range(2)]; mg_b = [Buf("mg") for _ in range(2)]
            pG = [ps("pG", [128, 512], F32) for _ in range(4)]; pG_b = [PBuf("pG") for _ in range(4)]
            pY = [ps("pY", [128, 512], F32) for _ in range(4)]; pY_b = [PBuf("pY") for _ in range(4)]
            tiles = [(s, qt) for s in range(NSEQ) for qt in range(NQT)]

            def load(ti):
                s, qt = tiles[ti]
                sl = ti % 2
                dma("sp", hTt[sl][:], self.hT_d[s, :, :, qt * 512:(qt + 1) * 512].rearrange("k p t -> p k t"), hT_b[sl], writes=[hT_b[sl]])
                dma("sp", oTt[sl][:], self.oT_d[s, :, :, qt * 512:(qt + 1) * 512].rearrange("k p t -> p k t"), oT_b[sl], writes=[oT_b[sl]])

            load(0)
            c = {"g": 0}
            for ti in range(len(tiles)):
                s, qt = tiles[ti]
                sl = ti % 2
                if ti + 1 < len(tiles):
                    load(ti + 1)
                for fc in range(8):
                    for b in range(4):
                        gi = c["g"] % 4; c["g"] += 1
                        col = b * D + fc * 128
                        for kc in range(8):
                            op("pe", lambda kc=kc, gi=gi, col=col: nc.tensor.matmul(pG[gi][:], lhsT=wg[:, kc, col:col + 128], rhs=hTt[sl][:, kc, :],
                                                                                   start=(kc == 0), stop=(kc == 7)),
                               reads=[wg_b, hT_b[sl]], writes=[pG_b[gi]], inc=(kc == 7))
                        chs = br_chunks[b]
                        for j, ch in enumerate(chs):
                            op("pe", lambda j=j, ch=ch, gi=gi: nc.tensor.matmul(pY[gi][:], lhsT=wbr[:, ch, fc * 128:(fc + 1) * 128], rhs=oTt[sl][:, ch, :],
                                                                                start=(j == 0), stop=(j == len(chs) - 1)),
                               reads=[wbr_b, oT_b[sl]], writes=[pY_b[gi]], inc=(j == len(chs) - 1))
                        op("act", lambda gi=gi, b=b: nc.scalar.activation(out=sg[b][:], in_=pG[gi][:], func=AF.Sigmoid), reads=[pG_b[gi]], writes=[sg_b[b]])
                        op("dve", lambda gi=gi, b=b: nc.vector.tensor_tensor(out=pr_[b][:], in0=pY[gi][:], in1=sg[b][:], op=ALU.mult),
                           reads=[pY_b[gi], sg_b[b]], writes=[pr_b[b]])
                    op("pool", lambda: nc.gpsimd.tensor_tensor(out=s2[0][:], in0=pr_[0][:], in1=pr_[1][:], op=ALU.add), reads=[pr_b[0], pr_b[1]], writes=[s2_b[0]])
                    op("pool", lambda: nc.gpsimd.tensor_tensor(out=s2[1][:], in0=pr_[2][:], in1=pr_[3][:], op=ALU.add), reads=[pr_b[2], pr_b[3]], writes=[s2_b[1]])
                    op("pool", lambda fc=fc: nc.gpsimd.tensor_tensor(out=mg[sl][:, fc, :], in0=s2[0][:], in1=s2[1][:], op=ALU.add),
                       reads=[s2_b[0], s2_b[1]], writes=[mg_b[sl]])
                dma("sp", self.mg_d[s, :, :, qt * 512:(qt + 1) * 512].rearrange("k p t -> p k t"), mg[sl][:], mg_b[sl], reads=[mg_b[sl]])

    def phase_MO(self, l, xin):
        nc, tr, W = self.nc, self.tr, self.W
        S, NSEQ, NQT = self.S, self.NSEQ, self.NQT
        op, dma = tr.op, tr.dma
        with ExitStack() as es:
            sb, ps = self.attn_common(es)
            wo = sb("wo", [128, 8, D], BF16); wo_b = Buf("wo")
            wfg = sb("wfg", [128, 8, DFF], BF16); wfu = sb("wfu", [128, 8, DFF], BF16); wf_b = Buf("wf")
            dma("pool", wo[:], W["w_o"][l].rearrange("(n p) d -> p n d", p=128), wo_b, writes=[wo_b])
            for kc in range(8):
                dma("pool", wfg[:, kc, :], W["w_ffn_gate"][l, kc * 128:(kc + 1) * 128, :], wf_b, writes=[wf_b])
                dma("pool", wfu[:, kc, :], W["w_ffn_up"][l, kc * 128:(kc + 1) * 128, :], wf_b, writes=[wf_b])
            g2 = sb("g2", [128, D], F32); const_b = Buf("const")
            ident = sb("ident", [128, 128], BF16)
            dma("sp", g2[:], W["norm2_g"][l].partition_broadcast(128), const_b, writes=[const_b])
            dma("sp", ident[:], self.ident_d, const_b, writes=[const_b])
            mgt = [sb("mgt", [128, 8, 512], BF16) for _ in range(2)]; mgt_b = [Buf("mgt") for _ in range(2)]
            xs = [sb("xs", [128, D], F32) for _ in range(2)]; xs_b = [Buf("xs") for _ in range(2)]
            xn = [sb("xn", [128, D], F32) for _ in range(2)]; xn_b = [Buf("xn") for _ in range(2)]
            junk = sb("junk", [128, D], F32); junk_b = Buf("junk")
            st1 = [sb("st1", [128, 4], F32) for _ in range(2)]; st1_b = [Buf("st1") for _ in range(2)]
            hb = [sb("hb", [128, D], BF16) for _ in range(2)]; hb_b = [Buf("hb") for _ in range(2)]
            hfT = [sb("hfT", [128, 8, 512], BF16) for _ in range(2)]; hfT_b = [Buf("hfT") for _ in range(2)]
            sl_ = [sb("sl", [128, 512], F32) for _ in range(2)]; sl_b = [Buf("sl") for _ in range(2)]
            at = [sb("at", [128, 2, 512], BF16) for _ in range(2)]; at_b = [Buf("at") for _ in range(2)]
            pO = ps("pO", [128, 1024], F32); pO_b = PBuf("pO")
            ptr = ps("ptr", [128, 8, 128], BF16); ptr_b = PBuf("ptr")
            pG = [ps("pG", [128, 512], F32) for _ in range(2)]; pG_b = [PBuf("pG") for _ in range(2)]
            pU = [ps("pU", [128, 512], F32) for _ in range(2)]; pU_b = [PBuf("pU") for _ in range(2)]
            tiles = [(s, qt) for s in range(NSEQ) for qt in range(NQT)]
            c = {"x": 0, "g": 0, "a": 0}

            def load_mg(ti):
                s, qt = tiles[ti]
                dma("sp", mgt[ti % 2][:], self.mg_d[s, :, :, qt * 512:(qt + 1) * 512].rearrange("k p t -> p k t"), mgt_b[ti % 2], writes=[mgt_b[ti % 2]])

            def load_x(ti, sub):
                s, qt = tiles[ti]
                xi = (ti * 4 + sub) % 2
                t0 = qt * 512 + sub * 128
                dma("sp", xs[xi][:], xin[s, t0:t0 + 128, :], xs_b[xi], writes=[xs_b[xi]])

            load_mg(0)
            load_x(0, 0)
            for ti in range(len(tiles)):
                s, qt = tiles[ti]
                sl = ti % 2
                if ti + 1 < len(tiles):
                    load_mg(ti + 1)
                pend_tr = []
                for sub in range(4):
                    xi = (ti * 4 + sub) % 2
                    t0 = qt * 512 + sub * 128
                    if sub < 3:
                        load_x(ti, sub + 1)
                    elif ti + 1 < len(tiles):
                        load_x(ti + 1, 0)
                    for half in range(2):
                        for kc in range(8):
                            op("pe", lambda kc=kc, half=half, sub=sub: nc.tensor.matmul(pO[:, half * 512:(half + 1) * 512], lhsT=mgt[sl][:, kc, sub * 128:(sub + 1) * 128],
                                                                                       rhs=wo[:, kc, half * 512:(half + 1) * 512], start=(kc == 0), stop=(kc == 7)),
                               reads=[mgt_b[sl], wo_b], writes=[pO_b], inc=(kc == 7))
                    op("dve", lambda xi=xi: nc.vector.tensor_tensor(out=xn[xi][:], in0=pO[:], in1=xs[xi][:], op=ALU.add), reads=[pO_b, xs_b[xi]], writes=[xn_b[xi]])
                    dma("sp", self.xm_d[s, t0:t0 + 128, :], xn[xi][:], xn_b[xi], reads=[xn_b[xi]])
                    op("act", lambda xi=xi: nc.scalar.activation(out=junk[:], in_=xn[xi][:], func=AF.Square, accum_out=st1[xi][:, 0:1]),
                       reads=[xn_b[xi]], writes=[junk_b, st1_b[xi]])
                    op("act", lambda xi=xi: nc.scalar.activation(out=st1[xi][:, 1:2], in_=st1[xi][:, 0:1], func=AF.Sqrt, bias=EPS, scale=1.0 / D),
                       reads=[st1_b[xi]], writes=[st1_b[xi]])
                    op("dve", lambda xi=xi: nc.vector.reciprocal(out=st1[xi][:, 2:3], in_=st1[xi][:, 1:2]), reads=[st1_b[xi]], writes=[st1_b[xi]])
                    op("dve", lambda xi=xi: nc.vector.scalar_tensor_tensor(out=hb[xi][:], in0=xn[xi][:], scalar=st1[xi][:, 2:3], in1=g2[:], op0=ALU.mult, op1=ALU.mult),
                       reads=[xn_b[xi], st1_b[xi], const_b], writes=[hb_b[xi]])
                    def tr_fn(xi=xi, sub=sub):
                        for kc in range(8):
                            op("pe", lambda kc=kc: nc.tensor.transpose(ptr[:, kc, :], hb[xi][:, kc * 128:(kc + 1) * 128], ident[:]),
                               reads=[hb_b[xi], const_b], writes=[ptr_b], inc=(kc == 7))
                        op("act", lambda: nc.scalar.copy(out=hfT[sl][:, :, sub * 128:(sub + 1) * 128], in_=ptr[:]), reads=[ptr_b], writes=[hfT_b[sl]])
                    if pend_tr:
                        pend_tr.pop()()
                    pend_tr.append(tr_fn)
                pend_tr.pop()()
                for fc in range(22):
                    gi = c["g"] % 2; c["g"] += 1
                    for kc in range(8):
                        op("pe", lambda kc=kc, gi=gi, fc=fc: nc.tensor.matmul(pG[gi][:], lhsT=wfg[:, kc, fc * 128:(fc + 1) * 128], rhs=hfT[sl][:, kc, :],
                                                                             start=(kc == 0), stop=(kc == 7)),
                           reads=[wf_b, hfT_b[sl]], writes=[pG_b[gi]], inc=(kc == 7))
                    for kc in range(8):
                        op("pe", lambda kc=kc, gi=gi, fc=fc: nc.tensor.matmul(pU[gi][:], lhsT=wfu[:, kc, fc * 128:(fc + 1) * 128], rhs=hfT[sl][:, kc, :],
                                                                             start=(kc == 0), stop=(kc == 7)),
                           reads=[wf_b, hfT_b[sl]], writes=[pU_b[gi]], inc=(kc == 7))
                    op("act", lambda gi=gi: nc.scalar.activation(out=sl_[gi][:], in_=pG[gi][:], func=AF.Silu), reads=[pG_b[gi]], writes=[sl_b[gi]])
                    ai = (fc // 2) % 2
                    op("dve", lambda gi=gi, ai=ai, fc=fc: nc.vector.tensor_tensor(out=at[ai][:, fc % 2, :], in0=pU[gi][:], in1=sl_[gi][:], op=ALU.mult),
                       reads=[pU_b[gi], sl_b[gi]], writes=[at_b[ai]])
                    if fc % 2 == 1:
                        f0 = fc - 1
                        dma("sp", self.aT_d[s, f0:f0 + 2, :, qt * 512:(qt + 1) * 512].rearrange("k p t -> p k t"), at[ai][:], at_b[ai], reads=[at_b[ai]])

    def phase_M2d(self, l, xout):
        nc, tr, W = self.nc, self.tr, self.W
        S, NSEQ, NQT = self.S, self.NSEQ, self.NQT
        op, dma = tr.op, tr.dma
        with ExitStack() as es:
            sb, ps = self.attn_common(es)
            wd = sb("wd", [128, 22, D], BF16); wd_b = Buf("wd")
            for j in range(2):
                dma("pool", wd[:, j * 11:(j + 1) * 11, :], W["w_ffn_down"][l, j * 1408:(j + 1) * 1408, :].rearrange("(n p) d -> p n d", p=128), wd_b, writes=[wd_b])
            aTt = [sb("aTt", [128, 22, 512], BF16) for _ in range(2)]; aT_b = [Buf("aTt") for _ in range(2)]
            xs = [sb("xs", [128, D], F32) for _ in range(2)]; xs_b = [Buf("xs") for _ in range(2)]
            xo = [sb("xo", [128, D], F32) for _ in range(2)]; xo_b = [Buf("xo") for _ in range(2)]
            pO = [ps("pO", [128, 1024], F32) for _ in range(2)]; pO_b = [PBuf("pO") for _ in range(2)]
            tiles = [(s, qt) for s in range(NSEQ) for qt in range(NQT)]

            def load_a(ti):
                s, qt = tiles[ti]
                dma("sp", aTt[ti % 2][:], self.aT_d[s, :, :, qt * 512:(qt + 1) * 512].rearrange("k p t -> p k t"), aT_b[ti % 2], writes=[aT_b[ti % 2]])

            def load_x(ti, sub):
                s, qt = tiles[ti]
                xi = (ti * 4 + sub) % 2
                t0 = qt * 512 + sub * 128
                dma("sp", xs[xi][:], self.xm_d[s, t0:t0 + 128, :], xs_b[xi], writes=[xs_b[xi]])

            load_a(0)
            load_x(0, 0)
            for ti in range(len(tiles)):
                s, qt = tiles[ti]
                sl = ti % 2
                if ti + 1 < len(tiles):
                    load_a(ti + 1)
                for sub in range(4):
                    xi = (ti * 4 + sub) % 2
                    t0 = qt * 512 + sub * 128
                    if sub < 3:
                        load_x(ti, sub + 1)
                    elif ti + 1 < len(tiles):
                        load_x(ti + 1, 0)
                    for half in range(2):
                        for fc in range(22):
                            op("pe", lambda fc=fc, half=half, sub=sub, xi=xi: nc.tensor.matmul(pO[xi][:, half * 512:(half + 1) * 512], lhsT=aTt[sl][:, fc, sub * 128:(sub + 1) * 128],
                                                                                              rhs=wd[:, fc, half * 512:(half + 1) * 512], start=(fc == 0), stop=(fc == 21)),
                               reads=[aT_b[sl], wd_b], writes=[pO_b[xi]], inc=(fc == 21))
                    op("dve", lambda xi=xi: nc.vector.tensor_tensor(out=xo[xi][:], in0=pO[xi][:], in1=xs[xi][:], op=ALU.add), reads=[pO_b[xi], xs_b[xi]], writes=[xo_b[xi]])
                    dma("sp", xout[s, t0:t0 + 128, :], xo[xi][:], xo_b[xi], reads=[xo_b[xi]])


def make_consts(S):
    bf = ml_dtypes.bfloat16
    pos = np.arange(S, dtype=np.float32)
    out = {}
    for half, name in ((32, "cs32"), (16, "cs16")):
        inv = np.power(np.float32(10000.0), -np.arange(half, dtype=np.float32) / np.float32(half)).astype(np.float32)
        ang = pos[:, None] * inv[None, :]
        out[name] = np.concatenate([np.cos(ang), np.sin(ang)], axis=1).astype(np.float32)
    masks = np.zeros((128, MASK_W), dtype=np.float32)
    k = np.arange(128)[:, None]
    for mk, (O, Wd, R_, dil) in MASK_DEF.items():
        j = np.arange(Wd)[None, :]
        d = k - j + O
        masks[:, MASK_COL[mk]:MASK_COL[mk] + Wd] = ((np.abs(d) <= R_) & (d % dil == 0)).astype(np.float32)
    out["masks"] = masks.astype(bf)
    out["ident"] = np.eye(128, dtype=np.float32).astype(bf)
    return out


_CACHE = {}
WEIGHT_NAMES = ["norm1_g", "w_in", "w_gate", "a_qnorm_g", "a_knorm_g", "a_lambda", "a_subln_g", "b_qnorm_g", "b_knorm_g",
                "c_qa_norm_g", "c_kva_norm_g", "c_w_uq", "c_w_ukv", "c_qnorm_g", "c_knorm_g", "d_qnorm_g", "d_knorm_g", "d_sink",
                "w_br_a", "w_br_b", "w_br_c", "w_br_d", "w_o", "norm2_g", "w_ffn_gate", "w_ffn_up", "w_ffn_down"]


def kernel(**inputs):
    xp = np.ascontiguousarray(np.asarray(inputs["x_prompt"], dtype=np.float32))
    xs = np.ascontiguousarray(np.asarray(inputs["x_sample"], dtype=np.float32))
    S = xp.shape[1]
    seqs = [xp[i] for i in range(xp.shape[0])] + [xs[i] for i in range(xs.shape[0])]
    n = 8
    doubles = [0, 1, 4, 5]
    assign = [[c, (8 + doubles.index(c)) if c in doubles else -1] for c in range(n)]
    key = ("full", S)
    if key not in _CACHE:
        _CACHE[key] = Builder(S=S, NSEQ=2, NL=2)
    b = _CACHE[key]
    consts = make_consts(S)
    wmap = {k: np.ascontiguousarray(np.asarray(inputs[k], dtype=np.float32)) for k in WEIGHT_NAMES}
    in_maps = []
    for c in range(n):
        m = dict(wmap)
        m.update(consts)
        second = seqs[assign[c][1]] if assign[c][1] >= 0 else np.zeros_like(seqs[0])
        m["x"] = np.stack([seqs[assign[c][0]], second], axis=0)
        in_maps.append(m)
    res = run_bass_kernel_spmd(b.nc, in_maps, core_ids=list(range(n)))
    outs = [None] * 12
    for c in range(n):
        y = np.asarray(res.results[c]["y"], dtype=np.float32)
        outs[assign[c][0]] = y[0]
        if assign[c][1] >= 0:
            outs[assign[c][1]] = y[1]
    y_prompt = np.stack(outs[0:4], axis=0).astype(np.float32)
    y_sample = np.stack(outs[4:12], axis=0).astype(np.float32)
    return (y_prompt, y_sample)
```

```python
import math
from collections import deque
from contextlib import ExitStack

import numpy as np
import ml_dtypes

import concourse.bass as bass
import concourse.mybir as mybir
from concourse.bass_utils import run_bass_kernel_spmd

F32, BF16 = mybir.dt.float32, mybir.dt.bfloat16
AF = mybir.ActivationFunctionType
ALU = mybir.AluOpType
AX = mybir.AxisListType

D = 1024
NIN = 5024
DFF = 2816
EPS = 1e-6
NQK = 41
VW = 1920
NOC = 14
import os as _os
VARIANT = int(_os.environ.get('PA_VARIANT', '1'))
MASK_DEF = {"d": (512, 1152, 128, 1), "b0": (512, 1152, 64, 1), "b1": (640, 1408, 256, 4), "b2": (1408, 2944, 1024, 16)}
MASK_COL = {}
_c = 0
for _k in ("d", "b0", "b1", "b2"):
    MASK_COL[_k] = _c
    _c += MASK_DEF[_k][1]
MASK_W = _c


def lambda_init(layer):
    return 0.8 - 0.6 * math.exp(-0.3 * layer)


class Buf:
    __slots__ = ("name", "w", "r", "dsem", "psum")

    def __init__(self, name, psum=False):
        self.name = name
        self.w = None
        self.r = {}
        self.dsem = None
        self.psum = psum


def PBuf(name):
    return Buf(name, psum=True)


class Tracker:
    def __init__(self, nc, es, ndma=60):
        self.nc = nc
        self.eng = {"pe": nc.tensor, "act": nc.scalar, "dve": nc.vector, "pool": nc.gpsimd, "sp": nc.sync}
        self.sems, self.val = [], []
        self.esem = {}
        for e in self.eng:
            self.esem[e] = self._new(es, "e_" + e)
        self.free_dma = [self._new(es, "d%d" % i) for i in range(ndma)]
        self.used_dma = []
        self.waited = {e: {} for e in self.eng}
        self.pending = {e: False for e in self.eng}
        self.n = 0
        self.log = None

    def _new(self, es, name):
        self.sems.append(es.enter_context(self.nc.semaphore(name)))
        self.val.append(0)
        return len(self.sems) - 1

    def _deps(self, reads, writes):
        raw, oth = {}, {}
        for b in reads:
            if b.w is not None:
                k, v = b.w
                if raw.get(k, 0) < v:
                    raw[k] = v
            if b.psum:
                for k, v in b.r.items():
                    if oth.get(k, 0) < v:
                        oth[k] = v
        for b in writes:
            if b.w is not None:
                k, v = b.w
                if oth.get(k, 0) < v:
                    oth[k] = v
            for k, v in b.r.items():
                if oth.get(k, 0) < v:
                    oth[k] = v
        return raw, oth

    def _wait(self, e, raw, oth):
        own = self.esem[e]
        w = self.waited[e]
        for k, v in raw.items():
            if k == own:
                if e == "pe":
                    continue
                assert v <= self.val[own], "own pending dep"
            if w.get(k, 0) < v:
                self.eng[e].wait_ge(self.sems[k], v)
                w[k] = v
                if self.log is not None:
                    self.log[e].append(("w", k, v))
        for k, v in oth.items():
            if k == own:
                continue
            if w.get(k, 0) < v:
                self.eng[e].wait_ge(self.sems[k], v)
                w[k] = v
                if self.log is not None:
                    self.log[e].append(("w", k, v))

    def op(self, e, fn, reads=(), writes=(), inc=True):
        raw, oth = self._deps(reads, writes)
        self._wait(e, raw, oth)
        ins = fn()
        self.n += 1
        k = self.esem[e]
        if inc:
            self.val[k] += 1
            ins.then_inc(self.sems[k], 1)
            if self.log is not None:
                self.log[e].append(("i", k, 1))
            st = self.val[k]
            self.pending[e] = False
        else:
            st = self.val[k] + 1
            self.pending[e] = True
        for b in reads:
            if b.r.get(k, 0) < st:
                b.r[k] = st
        for b in writes:
            b.w = (k, st)
            b.r = {}
        return ins

    def dma(self, q, out, in_, sb, reads=(), writes=()):
        raw, oth = self._deps(reads, writes)
        self._wait(q, raw, oth)
        if sb.dsem is None:
            sb.dsem = self.free_dma.pop()
            self.used_dma.append(sb.dsem)
        k = sb.dsem
        self.val[k] += 16
        self.eng[q].dma_start(out=out, in_=in_).then_inc(self.sems[k], 16)
        if self.log is not None:
            self.log[q].append(("i", k, 16))
        self.n += 1
        st = self.val[k]
        for b in reads:
            if b.r.get(k, 0) < st:
                b.r[k] = st
        for b in writes:
            b.w = (k, st)
            b.r = {}

    def barrier(self):
        for e in self.eng:
            assert not self.pending[e], e
        ks = list(self.esem.values()) + self.used_dma
        for e in self.eng:
            own = self.esem[e]
            w = self.waited[e]
            for k in ks:
                if k == own:
                    continue
                v = self.val[k]
                if v > 0 and w.get(k, 0) < v:
                    self.eng[e].wait_ge(self.sems[k], v)
                    w[k] = v
                    if self.log is not None:
                        self.log[e].append(("w", k, v))
        self.free_dma.extend(self.used_dma)
        self.used_dma = []


class Builder:
    def __init__(self, S=4096, NSEQ=2, NL=2, debug=False, phases=None):
        self.S, self.NSEQ, self.NL = S, NSEQ, NL
        self.NT = S // 128
        self.NQT = S // 512
        self.debug = debug
        self.phases = phases
        self.uid = 0
        nc = self.nc = bass.Bass("TRN2", target_bir_lowering=False)
        dt = nc.dram_tensor
        L = 2
        self.x = dt("x", [NSEQ, S, D], F32, kind="ExternalInput").ap()
        self.y = dt("y", [NSEQ, S, D], F32, kind="ExternalOutput").ap()
        W = {}
        for name, shape in [
            ("norm1_g", [L, D]), ("w_in", [L, D, NIN]), ("w_gate", [L, D, 4 * D]),
            ("a_qnorm_g", [L, 64]), ("a_knorm_g", [L, 64]), ("a_lambda", [L, 4, 64]), ("a_subln_g", [L, 128]),
            ("b_qnorm_g", [L, 3, 64]), ("b_knorm_g", [L, 3, 64]), ("c_qa_norm_g", [L, 256]),
            ("c_kva_norm_g", [L, 128]), ("c_w_uq", [L, 256, 768]), ("c_w_ukv", [L, 128, 1024]),
            ("c_qnorm_g", [L, 96]), ("c_knorm_g", [L, 96]), ("d_qnorm_g", [L, 64]), ("d_knorm_g", [L, 64]),
            ("d_sink", [L, 8]), ("w_br_a", [L, 512, D]), ("w_br_b", [L, 256, D]), ("w_br_c", [L, 512, D]),
            ("w_br_d", [L, 512, D]), ("w_o", [L, D, D]), ("norm2_g", [L, D]), ("w_ffn_gate", [L, D, DFF]),
            ("w_ffn_up", [L, D, DFF]), ("w_ffn_down", [L, DFF, D]),
        ]:
            W[name] = dt(name, shape, F32, kind="ExternalInput").ap()
        self.W = W
        self.ident_d = dt("ident", [128, 128], BF16, kind="ExternalInput").ap()
        self.cs32_d = dt("cs32", [S, 64], F32, kind="ExternalInput").ap()
        self.cs16_d = dt("cs16", [S, 32], F32, kind="ExternalInput").ap()
        self.masks_d = dt("masks", [128, MASK_W], BF16, kind="ExternalInput").ap()
        sk = "ExternalOutput" if debug else "Internal"
        self.hT_d = dt("hT_s", [NSEQ, 8, 128, S], BF16, kind=sk).ap()
        self.qk_d = dt("qk_s", [NSEQ, NQK, 128, S], BF16, kind=sk).ap()
        self.v_d = dt("v_s", [NSEQ, S, VW], BF16, kind=sk).ap()
        self.oT_d = dt("oT_s", [NSEQ, NOC, 128, S], BF16, kind=sk).ap()
        self.mg_d = dt("mg_s", [NSEQ, 8, 128, S], BF16, kind=sk).ap()
        self.aT_d = dt("aT_s", [NSEQ, 22, 128, S], BF16, kind=sk).ap()
        self.xm_d = dt("xm_s", [NSEQ, S, D], F32, kind=sk).ap()
        self.x1_d = dt("x1_s", [NSEQ, S, D], F32, kind=sk).ap()
        with ExitStack() as es:
            self.tr = Tracker(nc, es)
            if debug == "log":
                self.tr.log = {e: [] for e in self.tr.eng}
            self.build()

    def nm(self, s):
        self.uid += 1
        return "%s_%d" % (s, self.uid)

    def build(self):
        tr = self.tr
        pa_pre = None
        for l in range(self.NL):
            xin = self.x if l == 0 else self.x1_d
            xout = self.y if l == self.NL - 1 else self.x1_d
            if self.phases is not None:
                for ph, fn in [("PA", lambda: self.phase_PA(l, xin)), ("A", lambda: self.phase_A(l)),
                               ("C", lambda: self.phase_64(l, "c")), ("D", lambda: self.phase_64(l, "d")),
                               ("B", lambda: self.phase_64(l, "b")), ("M1", lambda: self.phase_M1g(l)),
                               ("MO", lambda: self.phase_MO(l, xin)), ("M2", lambda: self.phase_M2d(l, xout))]:
                    if ph in self.phases:
                        fn()
                        tr.barrier()
                continue
            self.phase_PA(l, xin, pre=pa_pre)
            tr.barrier()
            if pa_pre is not None:
                pa_pre[0].close()
                pa_pre = None
            self.phase_A(l); tr.barrier()
            self.phase_64(l, "c"); tr.barrier()
            self.phase_64(l, "b"); tr.barrier()
            with ExitStack() as esw:
                m1w = self.load_m1g_weights(esw, l)
                self.phase_64(l, "d"); tr.barrier()
                self.phase_M1g(l, pre=m1w); tr.barrier()
            self.phase_MO(l, xin); tr.barrier()
            if l + 1 < self.NL:
                esn = ExitStack()
                pa_pre = (esn,) + self.load_pa_weights(esn, l + 1)
            self.phase_M2d(l, xout); tr.barrier()
        tr.barrier()

    def load_pa_weights(self, es, l):
        nc, tr, W = self.nc, self.tr, self.W
        w_in = es.enter_context(nc.sbuf_tensor(self.nm("w_in"), [128, 8, NIN], BF16)); w_in_b = Buf("w_in")
        w_uq = es.enter_context(nc.sbuf_tensor(self.nm("w_uq"), [128, 2, 768], BF16))
        w_ukv = es.enter_context(nc.sbuf_tensor(self.nm("w_ukv"), [128, 1024], BF16)); w_c_b = Buf("w_c")
        for kc in range(8):
            tr.dma("pool", w_in[:, kc, :], W["w_in"][l, kc * 128:(kc + 1) * 128, :], w_in_b, writes=[w_in_b])
        for kc in range(2):
            tr.dma("pool", w_uq[:, kc, :], W["c_w_uq"][l, kc * 128:(kc + 1) * 128, :], w_c_b, writes=[w_c_b])
        tr.dma("pool", w_ukv[:], W["c_w_ukv"][l], w_c_b, writes=[w_c_b])
        return (w_in, w_in_b, w_uq, w_ukv, w_c_b)

    def load_m1g_weights(self, es, l):
        nc, tr, W = self.nc, self.tr, self.W
        wg = es.enter_context(nc.sbuf_tensor(self.nm("wg"), [128, 8, 4 * D], BF16)); wg_b = Buf("wg")
        wbr = es.enter_context(nc.sbuf_tensor(self.nm("wbr"), [128, NOC, D], BF16)); wbr_b = Buf("wbr")
        for kc in range(8):
            tr.dma("pool", wg[:, kc, :], W["w_gate"][l, kc * 128:(kc + 1) * 128, :], wg_b, writes=[wg_b])
        ci = 0
        br_chunks = []
        for name, n in (("w_br_a", 4), ("w_br_b", 2), ("w_br_c", 4), ("w_br_d", 4)):
            tr.dma("pool", wbr[:, ci:ci + n, :], W[name][l].rearrange("(n p) d -> p n d", p=128), wbr_b, writes=[wbr_b])
            br_chunks.append(list(range(ci, ci + n)))
            ci += n
        return (wg, wg_b, wbr, wbr_b, br_chunks)

    def phase_PA(self, l, xin, pre=None):
        nc, tr, W = self.nc, self.tr, self.W
        S, NT, NSEQ = self.S, self.NT, self.NSEQ
        op, dma = tr.op, tr.dma
        with ExitStack() as es:
            def sb(name, shape, dt):
                return es.enter_context(nc.sbuf_tensor(self.nm(name), shape, dt))

            def ps(name, shape, dt):
                return es.enter_context(nc.psum_tensor(self.nm(name), shape, dt))

            if pre is not None:
                _, w_in, w_in_b, w_uq, w_ukv, w_c_b = pre
            else:
                w_in, w_in_b, w_uq, w_ukv, w_c_b = self.load_pa_weights(es, l)
            GOFF = {}
            goff = 0
            glist = [("g1", W["norm1_g"][l], 1024), ("a_q", W["a_qnorm_g"][l], 64), ("a_k", W["a_knorm_g"][l], 64),
                     ("b_q", W["b_qnorm_g"][l].rearrange("g d -> (g d)"), 192),
                     ("b_k", W["b_knorm_g"][l].rearrange("g d -> (g d)"), 192),
                     ("c_qa", W["c_qa_norm_g"][l], 256), ("c_kva", W["c_kva_norm_g"][l], 128),
                     ("c_q", W["c_qnorm_g"][l], 96), ("c_k", W["c_knorm_g"][l], 96),
                     ("d_q", W["d_qnorm_g"][l], 64), ("d_k", W["d_knorm_g"][l], 64)]
            GWT = sum(g[2] for g in glist)
            gains = sb("gains", [128, GWT], F32); gains_b = Buf("gains")
            for name, ap, w in glist:
                GOFF[name] = goff
                dma("sp", gains[:, goff:goff + w], ap.partition_broadcast(128), gains_b, writes=[gains_b])
                goff += w

            def G(name, off, w):
                o = GOFF[name] + off
                return gains[:, o:o + w]

            cs32 = sb("cs32", [128, NT, 64], F32)
            cs16 = sb("cs16", [128, NT, 32], F32)
            ident = sb("ident", [128, 128], BF16)
            const_b = Buf("const")
            dma("sp", cs32[:], self.cs32_d.rearrange("(n p) f -> p n f", p=128), const_b, writes=[const_b])
            dma("sp", cs16[:], self.cs16_d.rearrange("(n p) f -> p n f", p=128), const_b, writes=[const_b])
            dma("sp", ident[:], self.ident_d, const_b, writes=[const_b])

            xt = [sb("xt", [128, D], F32) for _ in range(2)]; xt_b = [Buf("xt") for _ in range(2)]
            junk = sb("junk", [128, D], BF16); junk_b = Buf("junk")
            hx = None; hx_b = Buf("hx")
            st1 = [sb("st1", [128, 4], F32) for _ in range(2)]; st1_b = [Buf("st1") for _ in range(2)]
            hb = [sb("hb", [128, D], BF16) for _ in range(2)]; hb_b = [Buf("hb") for _ in range(2)]
            hT = [sb("hT", [128, 8, 128], BF16) for _ in range(2)]; hT_b = [Buf("hT") for _ in range(2)]
            NW = 6
            sq = [sb("sq", [128, 768], F32) for _ in range(2)]; sq_b = [Buf("sq") for _ in range(2)]
            stt = [sb("stt", [128, 32], F32) for _ in range(NW)]; stt_b = [Buf("stt") for _ in range(NW)]
            tt = [sb("tt", [128, 768], F32) for _ in range(NW)]; tt_b = [Buf("tt") for _ in range(NW)]
            rp = [sb("rp", [128, 4, 256], F32) for _ in range(2)]; rp_b = [Buf("rpa") for _ in range(2)]; rpb_b = [Buf("rpb") for _ in range(2)]
            QNW = 4736 + 1536
            qn = [sb("qn", [128, QNW], BF16) for _ in range(2)]; qn_b = [[Buf("qn%d" % i) for i in range(16)] for _ in range(2)]
            vst = [sb("vst", [128, VW], BF16) for _ in range(2)]; vst_b = [Buf("vst") for _ in range(2)]
            qTs = [sb("qTs", [128, 4, 128], BF16) for _ in range(3)]; qTs_b = [Buf("qTs") for _ in range(3)]
            cqn = [sb("cqn", [128, 384], BF16) for _ in range(2)]; cqn_b = [Buf("cqn") for _ in range(2)]
            cT = [sb("cT", [128, 3, 128], BF16) for _ in range(2)]; cT_b = [Buf("cT") for _ in range(2)]
            kcb = [sb("kcb", [128, 8, 96], F32) for _ in range(2)]; kcb_b = [Buf("kcb") for _ in range(2)]
            krp = [sb("krp", [128, 32], F32) for _ in range(2)]; krp_b = [Buf("krp") for _ in range(2)]

            NPJ = 4
            pj = [ps("pj", [128, 512], F32) for _ in range(NPJ)]; pj_b = [PBuf("pj") for _ in range(NPJ)]
            ptr = [ps("ptr", [128, 8, 128], BF16) for _ in range(2)]; ptr_b = [PBuf("ptr") for _ in range(2)]
            pq = ps("pq", [128, 1024], F32); pq_b = PBuf("pq")
            pkv = pq; pkv_b = pq_b

            cnt = {"w": 0, "tr": 0, "qs": 0}

            jobs = []

            def advance():
                for jb in reversed(jobs):
                    jb[1][jb[0]]()
                    jb[0] += 1
                while jobs and jobs[0][0] >= 4:
                    jobs.pop(0)

            def submit(src3, src_bufs, H, dh, g_ap, rope_spec, dst3, dst_bufs, after=None):
                wi = cnt["w"] % NW; cnt["w"] += 1
                w2 = wi % 2
                Wd = H * dh
                t3 = tt[wi][:, 0:Wd].rearrange("p (h d) -> p h d", h=H)
                sq3 = sq[w2][:, 0:Wd].rearrange("p (h d) -> p h d", h=H)
                tb, sb_, stb = tt_b[wi], sq_b[w2], stt_b[wi]
                op("act", lambda: nc.scalar.activation(out=sq3, in_=src3, func=AF.Square), reads=src_bufs, writes=[sb_])
                op("dve", lambda: nc.vector.tensor_tensor(out=t3, in0=src3, in1=g_ap.unsqueeze(1).to_broadcast([128, H, dh]), op=ALU.mult),
                   reads=list(src_bufs) + [gains_b], writes=[tb])
                if rope_spec is not None:
                    off, half, cos_ap, sin_ap = rope_spec
                    x1 = t3[:, :, off:off + half]
                    x2 = t3[:, :, off + half:off + 2 * half]
                    cb = cos_ap.unsqueeze(1).to_broadcast([128, H, half])
                    sbb = sin_ap.unsqueeze(1).to_broadcast([128, H, half])
                    n = H * half
                    r = rp[w2]
                    ra = r[:, 0, 0:n].rearrange("p (h d) -> p h d", h=H)
                    rb = r[:, 1, 0:n].rearrange("p (h d) -> p h d", h=H)
                    rc = r[:, 2, 0:n].rearrange("p (h d) -> p h d", h=H)
                    rd = r[:, 3, 0:n].rearrange("p (h d) -> p h d", h=H)
                    pa, pb = rp_b[w2], rpb_b[w2]

                def S1():
                    op("dve", lambda: nc.vector.tensor_reduce(out=stt[wi][:, 0:H], in_=sq3, axis=AX.X, op=ALU.add), reads=[sb_], writes=[stb])
                    if rope_spec is not None:
                        op("pool", lambda: nc.gpsimd.tensor_tensor(out=ra, in0=x1, in1=cb, op=ALU.mult), reads=[tb, const_b], writes=[pa])
                        op("dve", lambda: nc.vector.tensor_tensor(out=rb, in0=x2, in1=sbb, op=ALU.mult), reads=[tb, const_b], writes=[pb])
                        op("pool", lambda: nc.gpsimd.tensor_tensor(out=rd, in0=x1, in1=sbb, op=ALU.mult), reads=[tb, const_b], writes=[pa])
                        op("dve", lambda: nc.vector.tensor_tensor(out=rc, in0=x2, in1=cb, op=ALU.mult), reads=[tb, const_b], writes=[pb])

                def S2():
                    op("act", lambda: nc.scalar.activation(out=stt[wi][:, 8:8 + H], in_=stt[wi][:, 0:H], func=AF.Sqrt, bias=EPS, scale=1.0 / dh),
                       reads=[stb], writes=[stb])
                    if rope_spec is not None:
                        op("pool", lambda: nc.gpsimd.tensor_tensor(out=x1, in0=ra, in1=rb, op=ALU.subtract), reads=[pa, pb], writes=[tb])
                        op("dve", lambda: nc.vector.tensor_tensor(out=x2, in0=rc, in1=rd, op=ALU.add), reads=[pa, pb], writes=[tb])

                def S3():
                    op("dve", lambda: nc.vector.reciprocal(out=stt[wi][:, 16:16 + H], in_=stt[wi][:, 8:8 + H]), reads=[stb], writes=[stb])

                def S4():
                    if H > 2 and cnt["w"] % 2 == 0:
                        op("pool", lambda: nc.gpsimd.tensor_tensor(out=dst3, in0=t3, in1=stt[wi][:, 16:16 + H].unsqueeze(2).to_broadcast([128, H, dh]), op=ALU.mult),
                           reads=[tb, stb], writes=dst_bufs)
                    else:
                        for h in range(H):
                            op("act", lambda h=h: nc.scalar.activation(out=dst3[:, h, :], in_=t3[:, h, :], func=AF.Copy, scale=stt[wi][:, 16 + h:17 + h]),
                               reads=[tb, stb], writes=dst_bufs)
                    if after is not None:
                        after()

                advance()
                jobs.append([0, [S1, S2, S3, S4]])

            def transpose_out(s, t, src_tile, src_buf, col0, width, nchunk, chunk0):
                done = 0
                while done < nchunk:
                    n = min(4, nchunk - done)
                    pi = cnt["tr"] % 2; cnt["tr"] += 1
                    qi = cnt["qs"] % 3; cnt["qs"] += 1
                    for j in range(n):
                        c = col0 + (done + j) * width
                        op("pe", lambda j=j, c=c: nc.tensor.transpose(ptr[pi][0:width, j, :], src_tile[:, c:c + width], ident[:]),
                           reads=[src_buf, const_b], writes=[ptr_b[pi]], inc=(j == n - 1))
                    op("act", lambda: nc.scalar.copy(out=qTs[qi][0:width, 0:n, :], in_=ptr[pi][0:width, 0:n, :]),
                       reads=[ptr_b[pi]], writes=[qTs_b[qi]])
                    c0 = chunk0 + done
                    dma("sp", self.qk_d[s, c0:c0 + n, 0:width, t * 128:(t + 1) * 128].rearrange("n p t -> p n t"),
                        qTs[qi][0:width, 0:n, :], qTs_b[qi], reads=[qTs_b[qi]])
                    done += n

            segs = []
            segs.append(("qk", 0, 8, "a_q", 0, 0, 0))
            segs.append(("qk", 512, 8, "a_k", 0, 4, 1))
            segs.append(("v", 1024, 512, 0))
            for g in range(3):
                base = 1536 + g * 768
                segs.append(("qk", base, 4, "b_q", g * 64, 8 + 4 * g, 2 + 2 * g))
                segs.append(("qk", base + 256, 4, "b_k", g * 64, 10 + 4 * g, 3 + 2 * g))
                segs.append(("v", base + 512, 256, 512 + g * 256))
            segs.append(("cq", 3840))
            segs.append(("ckv", 4096))
            segs.append(("qk", 4256, 8, "d_q", 0, 20, 8))
            segs.append(("qk", 4768, 2, "d_k", 0, 24, 9))
            segs.append(("v", 4896, 128, 1792))
            bounds = [0, 512, 1024, 1536, 2048, 2560, 3072, 3584, 4096, 4256, 4768, 5024]
            qn_off = {}
            o = 0
            for sg in segs:
                if sg[0] == "qk":
                    qn_off[sg[6]] = o
                    o += sg[2] * 64
            QC_OFF = o
            KC_OFF = o + 768

            def prologue(gi):
                s, t = divmod(gi, NT)
                sl = gi % 2
                op("act", lambda: nc.scalar.activation(out=junk[:], in_=xt[sl][:], func=AF.Square, accum_out=st1[sl][:, 0:1]),
                   reads=[xt_b[sl]], writes=[junk_b, st1_b[sl]])
                op("act", lambda: nc.scalar.activation(out=st1[sl][:, 1:2], in_=st1[sl][:, 0:1], func=AF.Sqrt, bias=EPS, scale=1.0 / D),
                   reads=[st1_b[sl]], writes=[st1_b[sl]])
                op("dve", lambda: nc.vector.reciprocal(out=st1[sl][:, 2:3], in_=st1[sl][:, 1:2]), reads=[st1_b[sl]], writes=[st1_b[sl]])
                if VARIANT & 1:
                    op("dve", lambda: nc.vector.scalar_tensor_tensor(out=hb[sl][:], in0=xt[sl][:], scalar=st1[sl][:, 2:3], in1=G("g1", 0, D),
                                                                     op0=ALU.mult, op1=ALU.mult),
                       reads=[xt_b[sl], st1_b[sl], gains_b], writes=[hb_b[sl]])
                else:
                    op("act", lambda: nc.scalar.activation(out=hx[:], in_=xt[sl][:], func=AF.Copy, scale=st1[sl][:, 2:3]),
                       reads=[xt_b[sl], st1_b[sl]], writes=[hx_b])
                    op("pool", lambda: nc.gpsimd.tensor_tensor(out=hb[sl][:], in0=hx[:], in1=G("g1", 0, D), op=ALU.mult),
                       reads=[hx_b, gains_b], writes=[hb_b[sl]])
                pi = cnt["tr"] % 2; cnt["tr"] += 1
                for kc in range(8):
                    op("pe", lambda kc=kc: nc.tensor.transpose(ptr[pi][:, kc, :], hb[sl][:, kc * 128:(kc + 1) * 128], ident[:]),
                       reads=[hb_b[sl], const_b], writes=[ptr_b[pi]], inc=(kc == 7))
                op("act", lambda: nc.scalar.copy(out=hT[sl][:], in_=ptr[pi][:]), reads=[ptr_b[pi]], writes=[hT_b[sl]])
                dma("sp", self.hT_d[s, :, :, t * 128:(t + 1) * 128].rearrange("k p t -> p k t"), hT[sl][:], hT_b[sl], reads=[hT_b[sl]])

            def load_x(gi):
                s, t = divmod(gi, NT)
                dma("sp", xt[gi % 2][:], xin[s, t * 128:(t + 1) * 128, :], xt_b[gi % 2], writes=[xt_b[gi % 2]])

            def proc_seg(gi, sg, pjt, pjb, c0):
                s, t = divmod(gi, NT)
                sl = gi % 2
                q3 = gi % 2
                kind = sg[0]
                a = sg[1] - c0
                if kind == "v":
                    w, voff = sg[2], sg[3]
                    op("act", lambda: nc.scalar.copy(out=vst[sl][:, voff:voff + w], in_=pjt[:, a:a + w]), reads=[pjb], writes=[vst_b[sl]])
                elif kind == "qk":
                    _, col, H, gname, goff_, chunk0, slot = sg
                    Wd = H * 64
                    src3 = pjt[:, a:a + Wd].rearrange("p (h d) -> p h d", h=H)
                    qo = qn_off[slot]
                    d3 = qn[q3][:, qo:qo + Wd].rearrange("p (h d) -> p h d", h=H)
                    qb = qn_b[q3][slot]
                    submit(src3, [pjb], H, 64, G(gname, goff_, 64), (0, 32, cs32[:, t, 0:32], cs32[:, t, 32:64]), d3, [qb],
                           after=lambda: deferred.append((gi + 1, lambda: transpose_out(s, t, qn[q3], qb, qo, 128, Wd // 128, chunk0))))
                elif kind == "cq":
                    src3 = pjt[:, a:a + 256].rearrange("p (h d) -> p h d", h=1)
                    d3 = cqn[sl][:, 0:256].rearrange("p (h d) -> p h d", h=1)
                    submit(src3, [pjb], 1, 256, G("c_qa", 0, 256), None, d3, [cqn_b[sl]])
                elif kind == "ckv":
                    src3 = pjt[:, a:a + 128].rearrange("p (h d) -> p h d", h=1)
                    d3 = cqn[sl][:, 256:384].rearrange("p (h d) -> p h d", h=1)
                    op("act", lambda: nc.scalar.copy(out=krp[sl][:], in_=pjt[:, a + 128:a + 160]), reads=[pjb], writes=[krp_b[sl]])
                    submit(src3, [pjb], 1, 128, G("c_kva", 0, 128), None, d3, [cqn_b[sl]],
                           after=lambda: deferred.append((gi + 1, lambda: c_stage2(gi))))

            def c_stage2(gi):
                s, t = divmod(gi, NT)
                sl = gi % 2
                q3 = gi % 2
                pi = cnt["tr"] % 2; cnt["tr"] += 1
                for j in range(3):
                    op("pe", lambda j=j: nc.tensor.transpose(ptr[pi][:, j, :], cqn[sl][:, j * 128:(j + 1) * 128], ident[:]),
                       reads=[cqn_b[sl], const_b], writes=[ptr_b[pi]], inc=(j == 2))
                op("act", lambda: nc.scalar.copy(out=cT[sl][:], in_=ptr[pi][:, 0:3, :]), reads=[ptr_b[pi]], writes=[cT_b[sl]])
                for (c0, c1) in ((0, 512), (512, 768)):
                    for kc in range(2):
                        op("pe", lambda kc=kc, c0=c0, c1=c1: nc.tensor.matmul(pq[:, c0:c1], lhsT=cT[sl][:, kc, :], rhs=w_uq[:, kc, c0:c1],
                                                                              start=(kc == 0), stop=(kc == 1)),
                           reads=[cT_b[sl], w_c_b], writes=[pq_b], inc=(kc == 1))
                src3 = pq[:, 0:768].rearrange("p (h d) -> p h d", h=8)
                d3 = qn[q3][:, QC_OFF:QC_OFF + 768].rearrange("p (h d) -> p h d", h=8)
                qb = qn_b[q3][10]
                submit(src3, [pq_b], 8, 96, G("c_q", 0, 96), (64, 16, cs16[:, t, 0:16], cs16[:, t, 16:32]), d3, [qb],
                       after=lambda: deferred.append((gi + 2, lambda: transpose_out(s, t, qn[q3], qb, QC_OFF, 96, 8, 25))))
                for (c0, c1) in ((0, 512), (512, 1024)):
                    op("pe", lambda c0=c0, c1=c1: nc.tensor.matmul(pkv[:, c0:c1], lhsT=cT[sl][:, 2, :], rhs=w_ukv[:, c0:c1], start=True, stop=True),
                       reads=[cT_b[sl], w_c_b], writes=[pkv_b])
                kv3 = pkv[:].rearrange("p (h d) -> p h d", h=8)
                op("act", lambda: nc.scalar.copy(out=vst[sl][:, 1280:1792].rearrange("p (h d) -> p h d", h=8), in_=kv3[:, :, 64:128]),
                   reads=[pkv_b], writes=[vst_b[sl]])
                op("act", lambda: nc.scalar.copy(out=kcb[sl][:, :, 0:64], in_=kv3[:, :, 0:64]), reads=[pkv_b], writes=[kcb_b[sl]])
                op("dve", lambda: nc.vector.tensor_copy(out=kcb[sl][:, :, 64:96], in_=krp[sl][:].unsqueeze(1).to_broadcast([128, 8, 32])),
                   reads=[krp_b[sl]], writes=[kcb_b[sl]])
                d3k = qn[q3][:, KC_OFF:KC_OFF + 768].rearrange("p (h d) -> p h d", h=8)
                kb_ = qn_b[q3][11]
                submit(kcb[sl][:], [kcb_b[sl]], 8, 96, G("c_k", 0, 96), (64, 16, cs16[:, t, 0:16], cs16[:, t, 16:32]), d3k, [kb_],
                       after=lambda: deferred.append((gi + 2, lambda: transpose_out(s, t, qn[q3], kb_, KC_OFF, 96, 8, 33))))
                dma("sp", self.v_d[s, t * 128:(t + 1) * 128, :], vst[sl][:], vst_b[sl], reads=[vst_b[sl]])

            step = {"n": 0}

            class _Lag(deque):
                def append(self, item):
                    deque.append(self, (step["n"] + 3, item[1]))

            deferred = _Lag()

            def run_deferred(gi, limit=2):
                n = 0
                while deferred and n < limit and deferred[0][0] <= step["n"]:
                    _, fn = deferred.popleft()
                    fn()
                    n += 1

            NTILES = NSEQ * NT
            load_x(0)
            prologue(0)
            if NTILES > 1:
                load_x(1)
            ci_glob = 0
            for gi in range(NTILES):
                sl = gi % 2
                for ci in range(len(bounds) - 1):
                    c0, c1 = bounds[ci], bounds[ci + 1]
                    pi = ci_glob % NPJ; ci_glob += 1
                    for kc in range(8):
                        op("pe", lambda kc=kc, pi=pi, c0=c0, c1=c1: nc.tensor.matmul(pj[pi][:, 0:c1 - c0], lhsT=hT[sl][:, kc, :], rhs=w_in[:, kc, c0:c1],
                                                                                     start=(kc == 0), stop=(kc == 7)),
                           reads=[hT_b[sl], w_in_b], writes=[pj_b[pi]], inc=(kc == 7))
                    for sg in segs:
                        if c0 <= sg[1] < c1:
                            proc_seg(gi, sg, pj[pi], pj_b[pi], c0)
                    step["n"] += 1
                    run_deferred(gi)
                    if ci == 4 and gi + 1 < NTILES:
                        prologue(gi + 1)
                        if gi + 2 < NTILES:
                            load_x(gi + 2)
            while deferred or jobs:
                if jobs:
                    advance()
                step["n"] += 1
                run_deferred(0, limit=100)

    def attn_common(self, es):
        nc = self.nc

        def sb(name, shape, dt):
            return es.enter_context(nc.sbuf_tensor(self.nm(name), shape, dt))

        def ps(name, shape, dt):
            return es.enter_context(nc.psum_tensor(self.nm(name), shape, dt))
        return sb, ps

    def phase_A(self, l):
        nc, tr, W = self.nc, self.tr, self.W
        S, NT, NSEQ, NQT = self.S, self.NT, self.NSEQ, self.NQT
        op, dma = tr.op, tr.dma
        li = lambda_init(l)
        with ExitStack() as es:
            sb, ps = self.attn_common(es)
            ones = sb("ones", [128, 128], BF16); ones_b = Buf("ones")
            op("dve", lambda: nc.vector.memset(ones[:], 1.0), writes=[ones_b])
            lamt = sb("lamt", [128, 256], F32); sc_b = Buf("sc")
            prd = sb("prd", [128, 128], F32)
            sm = sb("sm", [128, 8], F32)
            gsub = sb("gsub", [128, 2], F32)
            dma("sp", lamt[:], W["a_lambda"][l].rearrange("a d -> (a d)").partition_broadcast(128), sc_b, writes=[sc_b])
            dma("sp", gsub[:, 0:1], W["a_subln_g"][l].rearrange("(p o) -> p o", o=1), sc_b, writes=[sc_b])
            op("dve", lambda: nc.vector.tensor_tensor(out=prd[:].rearrange("p (a d) -> p a d", a=2),
                                                      in0=lamt[:].rearrange("p (a b d) -> p a b d", a=2, b=2)[:, :, 0, :],
                                                      in1=lamt[:].rearrange("p (a b d) -> p a b d", a=2, b=2)[:, :, 1, :], op=ALU.mult),
               reads=[sc_b], writes=[sc_b])
            op("dve", lambda: nc.vector.tensor_reduce(out=sm[:, 0:2], in_=prd[:].rearrange("p (a d) -> p a d", a=2), axis=AX.X, op=ALU.add),
               reads=[sc_b], writes=[sc_b])
            op("act", lambda: nc.scalar.activation(out=sm[:, 2:4], in_=sm[:, 0:2], func=AF.Exp), reads=[sc_b], writes=[sc_b])
            op("dve", lambda: nc.vector.tensor_tensor(out=sm[:, 4:5], in0=sm[:, 3:4], in1=sm[:, 2:3], op=ALU.subtract), reads=[sc_b], writes=[sc_b])
            op("dve", lambda: nc.vector.tensor_scalar(out=sm[:, 5:6], in0=sm[:, 4:5], scalar1=-li, scalar2=None, op0=ALU.add), reads=[sc_b], writes=[sc_b])
            op("dve", lambda: nc.vector.tensor_scalar(out=gsub[:, 1:2], in0=gsub[:, 0:1], scalar1=1.0 - li, scalar2=None, op0=ALU.mult),
               reads=[sc_b], writes=[sc_b])
            nlam = sm[:, 5:6]
            gsc = gsub[:, 1:2]

            QT = [sb("QT", [128, S], BF16) for _ in range(2)]
            KT = [sb("KT", [128, S], BF16) for _ in range(2)]
            VA = [sb("VA", [128, NT, 128], BF16) for _ in range(2)]
            slot_b = [Buf("slot") for _ in range(2)]
            NE = 8
            E = [sb("E", [128, 512], BF16) for _ in range(NE)]; E_b = [Buf("E") for _ in range(NE)]
            acc = [[[sb("acc", [128, 512], F32) for _ in range(2)] for _ in range(2)] for _ in range(2)]
            acc_b = [[[Buf("acc") for _ in range(2)] for _ in range(2)] for _ in range(2)]
            accb = [[[sb("accb", [128, 512], BF16) for _ in range(2)] for _ in range(2)] for _ in range(2)]
            accb_b = [[[Buf("accb") for _ in range(2)] for _ in range(2)] for _ in range(2)]
            ev = [sb("ev", [128, 512], F32) for _ in range(2)]; ev_b = [Buf("ev") for _ in range(2)]
            rr = [sb("rr", [128, 512], F32) for _ in range(3)]; rr_b = [Buf("rr") for _ in range(3)]
            o32 = [sb("o32", [128, S], F32) for _ in range(2)]; o32_b = [[Buf("o32") for _ in range(NQT)] for _ in range(2)]
            osq = sb("osq", [128, 512], BF16); osq_b = Buf("osq")
            rms = sb("rms", [128, 512], F32); rms_b = Buf("rms")
            ob = [sb("ob", [128, 512], BF16) for _ in range(2)]; ob_b = [Buf("ob") for _ in range(2)]
            pS = [ps("pS", [128, 512], F32) for _ in range(4)]; pS_b = [PBuf("pS") for _ in range(4)]
            pU = [ps("pU", [128, 512], F32) for _ in range(3)]
            pU_b = [PBuf("pU") for _ in range(3)]
            evs = [sb("evs", [128, 512], F32) for _ in range(3)]; evs_b = [Buf("evs") for _ in range(3)]

            units = [(s, h) for s in range(NSEQ) for h in range(4)]

            def load(ui):
                s, h = units[ui]
                sl = ui % 2
                dma("sp", QT[sl][:], self.qk_d[s, h], slot_b[sl], writes=[slot_b[sl]])
                dma("sp", KT[sl][:], self.qk_d[s, 4 + h], slot_b[sl], writes=[slot_b[sl]])
                dma("sp", VA[sl][:], self.v_d[s].rearrange("(n p) f -> p n f", p=128)[:, :, h * 128:(h + 1) * 128], slot_b[sl], writes=[slot_b[sl]])

            def subln(ui):
                s, h = units[ui]
                u2 = ui % 2
                for qt in range(NQT):
                    q0 = qt * 512
                    osl = o32[u2][:, q0:q0 + 512]
                    ob_ = o32_b[u2][qt]
                    op("pool", lambda osl=osl: nc.gpsimd.tensor_tensor(out=osq[:], in0=osl, in1=osl, op=ALU.mult), reads=[ob_], writes=[osq_b])
                    si = cs["s"] % 4; cs["s"] += 1
                    op("pe", lambda si=si: nc.tensor.matmul(pS[si][:], lhsT=ones[:], rhs=osq[:], start=True, stop=True), reads=[osq_b, ones_b], writes=[pS_b[si]])
                    op("act", lambda si=si: nc.scalar.activation(out=rms[:], in_=pS[si][:], func=AF.Sqrt, bias=EPS, scale=1.0 / 128), reads=[pS_b[si]], writes=[rms_b])
                    op("dve", lambda: nc.vector.reciprocal(out=rr[2][:], in_=rms[:]), reads=[rms_b], writes=[rr_b[2]])
                    oi = cs["o"] % 2; cs["o"] += 1
                    op("dve", lambda oi=oi, osl=osl: nc.vector.scalar_tensor_tensor(out=ob[oi][:], in0=osl, scalar=gsc, in1=rr[2][:], op0=ALU.mult, op1=ALU.mult),
                       reads=[ob_, rr_b[2], sc_b], writes=[ob_b[oi]])
                    dma("sp", self.oT_d[s, h, :, q0:q0 + 512], ob[oi][:], ob_b[oi], reads=[ob_b[oi]])

            load(0)
            cs = {"s": 0, "e": 0, "o": 0, "q": 0}
            LA = 3
            for ui in range(len(units)):
                s, h = units[ui]
                sl = ui % 2
                u2 = ui % 2
                if ui + 1 < len(units):
                    load(ui + 1)
                for qt in range(NQT):
                    q0 = qt * 512
                    a2 = cs["q"] % 2; cs["q"] += 1
                    ids = []
                    for i in range(NT + LA):
                        if i < NT:
                            rec = []
                            for c in range(2):
                                si = cs["s"] % 4; cs["s"] += 1
                                ei = cs["e"] % NE; cs["e"] += 1
                                p0 = c * 64
                                op("pe", lambda si=si, p0=p0, i=i: nc.tensor.matmul(pS[si][:], lhsT=KT[sl][p0:p0 + 64, i * 128:(i + 1) * 128],
                                                                                   rhs=QT[sl][p0:p0 + 64, q0:q0 + 512], start=True, stop=True),
                                   reads=[slot_b[sl]], writes=[pS_b[si]])
                                op("act", lambda si=si, ei=ei: nc.scalar.activation(out=E[ei][:], in_=pS[si][:], func=AF.Exp, scale=0.125),
                                   reads=[pS_b[si]], writes=[E_b[ei]])
                                rec.append(ei)
                                if c == 1:
                                    continue
                                if i % 4 == 3:
                                    me, eng, ak = "dve", nc.vector, (1, 0)
                                elif i % 4 == 1:
                                    me, eng, ak = "pool", nc.gpsimd, (0, 1)
                                else:
                                    me, eng, ak = "pool", nc.gpsimd, (0, 0)
                                A_ = acc[a2][ak[0]][ak[1]]; Ab = acc_b[a2][ak[0]][ak[1]]
                                mine = [x for x in range(NT) if (x % 4 == 3) == (i % 4 == 3) and (x % 4 == 1) == (i % 4 == 1)]
                                if i == mine[0]:
                                    op(me, lambda eng=eng, A_=A_, ei=ei: eng.tensor_copy(out=A_[:], in_=E[ei][:]), reads=[E_b[ei]], writes=[Ab])
                                elif i != mine[-1]:
                                    op(me, lambda eng=eng, A_=A_, ei=ei: eng.tensor_tensor(out=A_[:], in0=A_[:], in1=E[ei][:], op=ALU.add),
                                       reads=[E_b[ei], Ab], writes=[Ab])
                                else:
                                    op(me, lambda eng=eng, A_=A_, ei=ei, ak=ak: eng.tensor_tensor(out=accb[a2][ak[0]][ak[1]][:], in0=A_[:], in1=E[ei][:], op=ALU.add),
                                       reads=[E_b[ei], Ab], writes=[accb_b[a2][ak[0]][ak[1]]])
                            ids.append(rec)
                        if i >= LA:
                            j = i - LA
                            for c in range(2):
                                ei = ids[j][c]
                                op("pe", lambda c=c, ei=ei, j=j: nc.tensor.matmul(pU[c][:], lhsT=VA[sl][:, j, :], rhs=E[ei][:], start=(j == 0), stop=(j == NT - 1)),
                                   reads=[E_b[ei], slot_b[sl]], writes=[pU_b[c]], inc=(c == 0))
                                if c == 1:
                                    op("pe", lambda ei=ei, j=j: nc.tensor.matmul(pU[2][:], lhsT=ones[:], rhs=E[ei][:], start=(j == 0), stop=(j == NT - 1)),
                                       reads=[E_b[ei], ones_b], writes=[pU_b[2]])
                    op("dve", lambda: nc.vector.tensor_copy(out=evs[0][:], in_=pU[0][:]), reads=[pU_b[0]], writes=[evs_b[0]])
                    op("act", lambda: nc.scalar.copy(out=evs[1][:], in_=pU[1][:]), reads=[pU_b[1]], writes=[evs_b[1]])
                    op("dve", lambda: nc.vector.tensor_copy(out=evs[2][:], in_=pU[2][:]), reads=[pU_b[2]], writes=[evs_b[2]])
                    si = cs["s"] % 4; cs["s"] += 1
                    for n_, ak in enumerate(((0, 0), (0, 1), (1, 0))):
                        op("pe", lambda si=si, ak=ak, n_=n_: nc.tensor.matmul(pS[si][:], lhsT=ones[:], rhs=accb[a2][ak[0]][ak[1]][:], start=(n_ == 0), stop=(n_ == 2)),
                           reads=[accb_b[a2][ak[0]][ak[1]], ones_b], writes=[pS_b[si]], inc=(n_ == 2))
                    op("dve", lambda si=si: nc.vector.reciprocal(out=rr[0][:], in_=pS[si][:]), reads=[pS_b[si]], writes=[rr_b[0]])
                    op("dve", lambda: nc.vector.reciprocal(out=rr[1][:], in_=evs[2][:]), reads=[evs_b[2]], writes=[rr_b[1]])
                    for c in range(2):
                        op("dve", lambda c=c: nc.vector.tensor_tensor(out=ev[c][:], in0=evs[c][:], in1=rr[c][:], op=ALU.mult),
                           reads=[evs_b[c], rr_b[c]], writes=[ev_b[c]])
                    op("dve", lambda: nc.vector.scalar_tensor_tensor(out=o32[u2][:, q0:q0 + 512], in0=ev[1][:], scalar=nlam, in1=ev[0][:], op0=ALU.mult, op1=ALU.add),
                       reads=[ev_b[0], ev_b[1], sc_b], writes=[o32_b[u2][qt]])
                    if qt == 0 and ui > 0:
                        subln(ui - 1)
            subln(len(units) - 1)

    def phase_64(self, l, mixer):
        nc, tr, W = self.nc, self.tr, self.W
        S, NT, NSEQ, NQT = self.S, self.NT, self.NSEQ, self.NQT
        op, dma = tr.op, tr.dma
        with ExitStack() as es:
            sb, ps = self.attn_common(es)
            NE = 12
            NPS = 6
            E = [sb("E", [128, 512], BF16) for _ in range(NE)]; E_b = [Buf("E") for _ in range(NE)]
            R = [sb("R", [128, 512], F32) for _ in range(2)]; R_b = [Buf("R") for _ in range(2)]
            ob = [sb("ob", [128, 512], BF16) for _ in range(2)]; ob_b = [Buf("ob") for _ in range(2)]
            pS = [ps("pS", [128, 512], F32) for _ in range(NPS)]; pS_b = [PBuf("pS") for _ in range(NPS)]
            pT = [ps("pT", [128, 512], F32) for _ in range(2)]; pT_b = [PBuf("pT") for _ in range(2)]
            cs = {"s": 0, "e": 0, "t": 0, "m": 0}
            const_b = Buf("const")
            if mixer in ("b", "d"):
                masks = sb("masks", [128, MASK_W], BF16)
                dma("sp", masks[:], self.masks_d, const_b, writes=[const_b])
            if mixer == "d":
                es8 = sb("es8", [128, 16], F32)
                dma("sp", es8[:, 0:8], W["d_sink"][l].partition_broadcast(128), const_b, writes=[const_b])
                op("act", lambda: nc.scalar.activation(out=es8[:, 8:16], in_=es8[:, 0:8], func=AF.Exp), reads=[const_b], writes=[const_b])

            def band(mk, qt):
                O, Wd, R_, dil = MASK_DEF[mk]
                q0 = qt * 512
                res = []
                for kb in range(NT):
                    dl = kb * 128 - q0
                    if -R_ - 127 <= dl <= R_ + 511:
                        j0 = MASK_COL[mk] + O - dl
                        res.append((kb, j0))
                return res

            pipe = deque()
            LA = 5

            def pipe_flush(keep):
                while len(pipe) > keep:
                    pipe.popleft()()

            def run_head(terms, parity, scale, sink_ap, out_ap, tail_fn=None):
                ti = cs["t"] % 2; cs["t"] += 1
                n = len(terms)
                for i in range(n):
                    kT, qT, va, mk, bufs = terms[i]
                    si = cs["s"] % NPS; cs["s"] += 1
                    ei = cs["e"] % NE; cs["e"] += 1
                    op("pe", lambda si=si, kT=kT, qT=qT: nc.tensor.matmul(pS[si][:], lhsT=kT, rhs=qT, start=True, stop=True),
                       reads=bufs, writes=[pS_b[si]])
                    op("act", lambda si=si, ei=ei: nc.scalar.activation(out=E[ei][:], in_=pS[si][:], func=AF.Exp, scale=scale),
                       reads=[pS_b[si]], writes=[E_b[ei]])
                    if mk is not None:
                        me = "pool" if cs["m"] % 4 == 0 else "dve"
                        cs["m"] += 1
                        eng = nc.gpsimd if me == "pool" else nc.vector
                        op(me, lambda eng=eng, ei=ei, mk=mk: eng.tensor_tensor(out=E[ei][:], in0=E[ei][:], in1=mk, op=ALU.mult),
                           reads=[E_b[ei], const_b], writes=[E_b[ei]])

                    def back(i=i, ei=ei, va=va, bufs=bufs):
                        op("pe", lambda: nc.tensor.matmul(pT[ti][:], lhsT=va, rhs=E[ei][:], start=(i == 0), stop=(i == n - 1)),
                           reads=[E_b[ei]] + list(bufs), writes=[pT_b[ti]])
                        if i == n - 1:
                            if tail_fn is not None:
                                tail_fn(ti)
                            else:
                                tail(ti, parity, sink_ap, out_ap)
                    pipe.append(back)
                    pipe_flush(LA)

            def tail(ti, parity, sink_ap, out_ap):
                ur = (0, 64) if parity == 0 else (64, 128)
                lr = (64, 128) if parity == 0 else (0, 64)
                T = pT[ti]
                Rt = R[ti]
                if sink_ap is not None:
                    op("dve", lambda: nc.vector.tensor_scalar(out=Rt[lr[0]:lr[1], :], in0=T[lr[0]:lr[1], :], scalar1=sink_ap(lr), scalar2=None, op0=ALU.add),
                       reads=[pT_b[ti], const_b], writes=[R_b[ti]])
                    op("dve", lambda: nc.vector.reciprocal(out=Rt[lr[0]:lr[1], :], in_=Rt[lr[0]:lr[1], :]), reads=[R_b[ti]], writes=[R_b[ti]])
                else:
                    op("dve", lambda: nc.vector.reciprocal(out=Rt[lr[0]:lr[1], :], in_=T[lr[0]:lr[1], :]), reads=[pT_b[ti]], writes=[R_b[ti]])
                op("dve", lambda: nc.vector.tensor_tensor(out=ob[ti][ur[0]:ur[1], :], in0=T[ur[0]:ur[1], :], in1=Rt[lr[0]:lr[1], :], op=ALU.mult),
                   reads=[pT_b[ti], R_b[ti]], writes=[ob_b[ti]])
                dma("sp", out_ap(ur), ob[ti][ur[0]:ur[1], :], ob_b[ti], reads=[ob_b[ti]])

            def fill_ones(va_tile, buf, nhead_slots):
                for p in range(nhead_slots):
                    c0 = 64 if p % 2 == 0 else 0
                    op("pool", lambda p=p, c0=c0: nc.gpsimd.memset(va_tile[:, :, p, c0:c0 + 64], 1.0), writes=[buf])

            if mixer == "c":
                scale = 96 ** -0.5
                QT = [sb("QT", [128, S], BF16) for _ in range(2)]
                KT = [sb("KT", [128, S], BF16) for _ in range(2)]
                VA = [sb("VA", [128, NT, 2, 128], BF16) for _ in range(2)]
                slot_b = [Buf("slot") for _ in range(2)]
                for i in range(2):
                    fill_ones(VA[i], slot_b[i], 2)
                units = [(s, h) for s in range(NSEQ) for h in range(8)]

                def load(ui):
                    s, h = units[ui]
                    sl = ui % 2
                    p = h % 2
                    dma("sp", QT[sl][0:96, :], self.qk_d[s, 25 + h, 0:96, :], slot_b[sl], writes=[slot_b[sl]])
                    dma("sp", KT[sl][0:96, :], self.qk_d[s, 33 + h, 0:96, :], slot_b[sl], writes=[slot_b[sl]])
                    vc = 1280 + h * 64
                    c0 = 0 if p == 0 else 64
                    dma("sp", VA[sl][:, :, p, c0:c0 + 64], self.v_d[s].rearrange("(n p) f -> p n f", p=128)[:, :, vc:vc + 64], slot_b[sl], writes=[slot_b[sl]])

                load(0)
                for ui in range(len(units)):
                    s, h = units[ui]
                    sl = ui % 2
                    p = h % 2
                    if ui + 1 < len(units):
                        pipe_flush(0)
                        load(ui + 1)
                    for qt in range(NQT):
                        q0 = qt * 512
                        terms = [(KT[sl][0:96, kb * 128:(kb + 1) * 128], QT[sl][0:96, q0:q0 + 512], VA[sl][:, kb, p, :], None, [slot_b[sl]])
                                 for kb in range(NT)]
                        run_head(terms, p, scale, None, lambda ur, s=s, h=h, q0=q0: self.oT_d[s, 6 + h // 2, ur[0]:ur[1], q0:q0 + 512])
            elif mixer == "d":
                scale = 0.125
                QT = [sb("QT", [128, S], BF16) for _ in range(2)]
                KT = [sb("KT", [128, S], BF16) for _ in range(2)]
                VA = [sb("VA", [128, NT, 2, 128], BF16) for _ in range(2)]
                slot_b = [Buf("slot") for _ in range(2)]
                for i in range(2):
                    fill_ones(VA[i], slot_b[i], 2)
                units = [(s, pr) for s in range(NSEQ) for pr in range(4)]

                def load(ui):
                    s, pr = units[ui]
                    sl = ui % 2
                    kvh = pr // 2
                    dma("sp", QT[sl][:], self.qk_d[s, 20 + pr], slot_b[sl], writes=[slot_b[sl]])
                    for hh in range(2):
                        dma("sp", KT[sl][hh * 64:hh * 64 + 64, :], self.qk_d[s, 24, kvh * 64:kvh * 64 + 64, :], slot_b[sl], writes=[slot_b[sl]])
                    vc = 1792 + kvh * 64
                    vsrc = self.v_d[s].rearrange("(n p) f -> p n f", p=128)[:, :, vc:vc + 64]
                    dma("sp", VA[sl][:, :, 0, 0:64], vsrc, slot_b[sl], writes=[slot_b[sl]])
                    dma("sp", VA[sl][:, :, 1, 64:128], vsrc, slot_b[sl], writes=[slot_b[sl]])

                load(0)
                for ui in range(len(units)):
                    s, pr = units[ui]
                    sl = ui % 2
                    kvh = pr // 2
                    if ui + 1 < len(units):
                        pipe_flush(0)
                        load(ui + 1)
                    for qt in range(NQT):
                        q0 = qt * 512
                        bd = band("d", qt)
                        for p in range(2):
                            h = pr * 2 + p
                            terms = [(KT[sl][p * 64:p * 64 + 64, kb * 128:(kb + 1) * 128], QT[sl][p * 64:p * 64 + 64, q0:q0 + 512],
                                      VA[sl][:, kb, p, :], masks[:, j0:j0 + 512], [slot_b[sl]]) for kb, j0 in bd]
                            run_head(terms, p, scale, lambda lr, h=h: es8[lr[0]:lr[1], 8 + h:9 + h],
                                     lambda ur, s=s, pr=pr, q0=q0: self.oT_d[s, 10 + pr, ur[0]:ur[1], q0:q0 + 512])
            else:
                scale = 0.125
                QT = [sb("QT", [128, S], BF16) for _ in range(2)]
                KT = [sb("KT", [128, S], BF16) for _ in range(2)]
                VA = [sb("VA", [128, NT, 2, 128], BF16) for _ in range(2)]
                slot_b = [Buf("slot") for _ in range(2)]
                for i in range(2):
                    fill_ones(VA[i], slot_b[i], 2)
                bacc = [sb("bacc", [128, S], F32) for _ in range(2)]
                bacc_b = [[Buf("bacc") for _ in range(NQT)] for _ in range(2)]
                subunits = [(s, pr, g) for s in range(NSEQ) for pr in range(2) for g in range(3)]

                def load(k):
                    s, pr, g = subunits[k]
                    sl = k % 2
                    dma("sp", QT[sl][:], self.qk_d[s, 8 + 4 * g + pr], slot_b[sl], writes=[slot_b[sl]])
                    dma("sp", KT[sl][:], self.qk_d[s, 10 + 4 * g + pr], slot_b[sl], writes=[slot_b[sl]])
                    for p in range(2):
                        vc = 512 + g * 256 + (pr * 2 + p) * 64
                        c0 = 0 if p == 0 else 64
                        dma("sp", VA[sl][:, :, p, c0:c0 + 64], self.v_d[s].rearrange("(n p) f -> p n f", p=128)[:, :, vc:vc + 64],
                            slot_b[sl], writes=[slot_b[sl]])

                def tail_b(ti, p, g, qt, s, pr):
                    q0 = qt * 512
                    A_ = bacc[p][:, q0:q0 + 512]
                    Ab = bacc_b[p][qt]
                    if g == 0:
                        op("dve", lambda: nc.vector.tensor_copy(out=A_, in_=pT[ti][:]), reads=[pT_b[ti]], writes=[Ab])
                        return
                    if g == 1:
                        op("dve", lambda: nc.vector.tensor_tensor(out=A_, in0=pT[ti][:], in1=A_, op=ALU.add), reads=[pT_b[ti], Ab], writes=[Ab])
                        return
                    ur = (0, 64) if p == 0 else (64, 128)
                    lr = (64, 128) if p == 0 else (0, 64)
                    T = pT[ti]
                    op("dve", lambda: nc.vector.tensor_tensor(out=R[ti][lr[0]:lr[1], :], in0=T[lr[0]:lr[1], :], in1=bacc[p][lr[0]:lr[1], q0:q0 + 512], op=ALU.add),
                       reads=[pT_b[ti], Ab], writes=[R_b[ti]])
                    op("dve", lambda: nc.vector.reciprocal(out=R[ti][lr[0]:lr[1], :], in_=R[ti][lr[0]:lr[1], :]), reads=[R_b[ti]], writes=[R_b[ti]])
                    op("dve", lambda: nc.vector.tensor_tensor(out=T[ur[0]:ur[1], :], in0=T[ur[0]:ur[1], :], in1=bacc[p][ur[0]:ur[1], q0:q0 + 512], op=ALU.add),
                       reads=[pT_b[ti], Ab], writes=[pT_b[ti]])
                    op("dve", lambda: nc.vector.tensor_tensor(out=ob[ti][ur[0]:ur[1], :], in0=T[ur[0]:ur[1], :], in1=R[ti][lr[0]:lr[1], :], op=ALU.mult),
                       reads=[pT_b[ti], R_b[ti]], writes=[ob_b[ti]])
                    dma("sp", self.oT_d[s, 4 + pr, ur[0]:ur[1], q0:q0 + 512], ob[ti][ur[0]:ur[1], :], ob_b[ti], reads=[ob_b[ti]])

                load(0)
                for k in range(len(subunits)):
                    s, pr, g = subunits[k]
                    sl = k % 2
                    if k + 1 < len(subunits):
                        pipe_flush(0)
                        load(k + 1)
                    for qt in range(NQT):
                        q0 = qt * 512
                        for p in range(2):
                            terms = [(KT[sl][p * 64:p * 64 + 64, kb * 128:(kb + 1) * 128], QT[sl][p * 64:p * 64 + 64, q0:q0 + 512],
                                      VA[sl][:, kb, p, :], masks[:, j0:j0 + 512], [slot_b[sl]]) for kb, j0 in band("b%d" % g, qt)]
                            run_head(terms, p, scale, None, None,
                                     tail_fn=lambda ti, p=p, g=g, qt=qt, s=s, pr=pr: tail_b(ti, p, g, qt, s, pr))
            pipe_flush(0)

    def phase_M1g(self, l, pre=None):
        nc, tr, W = self.nc, self.tr, self.W
        S, NSEQ, NQT = self.S, self.NSEQ, self.NQT
        op, dma = tr.op, tr.dma
        with ExitStack() as es:
            sb, ps = self.attn_common(es)
            wg, wg_b, wbr, wbr_b, br_chunks = pre if pre is not None else self.load_m1g_weights(es, l)
            hTt = [sb("hTt", [128, 8, 512], BF16) for _ in range(2)]; hT_b = [Buf("hTt") for _ in range(2)]
            oTt = [sb("oTt", [128, NOC, 512], BF16) for _ in range(2)]; oT_b = [Buf("oTt") for _ in range(2)]
            sg = [sb("sg", [128, 512], F32) for _ in range(4)]; sg_b = [Buf("sg") for _ in range(4)]
            pr_ = [sb("pr", [128, 512], F32) for _ in range(4)]; pr_b = [Buf("pr") for _ in range(4)]
            s2 = [sb("s2", [128, 512], F32) for _ in range(2)]; s2_b = [Buf("s2") for _ in range(2)]
            mg = [sb("mg", [128, 8, 512], BF16) for _ in range(2)]; mg_b = [Buf("mg") for _ in range(2)]
            pG = [ps("pG", [128, 512], F32) for _ in range(4)]; pG_b = [PBuf("pG") for _ in range(4)]
            pY = [ps("pY", [128, 512], F32) for _ in range(4)]; pY_b = [PBuf("pY") for _ in range(4)]
            tiles = [(s, qt) for s in range(NSEQ) for qt in range(NQT)]

            def load(ti):
                s, qt = tiles[ti]
                sl = ti % 2
                dma("sp", hTt[sl][:], self.hT_d[s, :, :, qt * 512:(qt + 1) * 512].rearrange("k p t -> p k t"), hT_b[sl], writes=[hT_b[sl]])
                dma("sp", oTt[sl][:], self.oT_d[s, :, :, qt * 512:(qt + 1) * 512].rearrange("k p t -> p k t"), oT_b[sl], writes=[oT_b[sl]])

            load(0)
            c = {"g": 0}
            for ti in range(len(tiles)):
                s, qt = tiles[ti]
                sl = ti % 2
                if ti + 1 < len(tiles):
                    load(ti + 1)
                for fc in range(8):
                    for b in range(4):
                        gi = c["g"] % 4; c["g"] += 1
                        col = b * D + fc * 128
                        for kc in range(8):
                            op("pe", lambda kc=kc, gi=gi, col=col: nc.tensor.matmul(pG[gi][:], lhsT=wg[:, kc, col:col + 128], rhs=hTt[sl][:, kc, :],
                                                                                   start=(kc == 0), stop=(kc == 7)),
                               reads=[wg_b, hT_b[sl]], writes=[pG_b[gi]], inc=(kc == 7))
                        chs = br_chunks[b]
                        for j, ch in enumerate(chs):
                            op("pe", lambda j=j, ch=ch, gi=gi: nc.tensor.matmul(pY[gi][:], lhsT=wbr[:, ch, fc * 128:(fc + 1) * 128], rhs=oTt[sl][:, ch, :],
                                                                                start=(j == 0), stop=(j == len(chs) - 1)),
                               reads=[wbr_b, oT_b[sl]], writes=[pY_b[gi]], inc=(j == len(chs) - 1))
                        op("act", lambda gi=gi, b=b: nc.scalar.activation(out=sg[b][:], in_=pG[gi][:], func=AF.Sigmoid), reads=[pG_b[gi]], writes=[sg_b[b]])
                        op("dve", lambda gi=gi, b=b: nc.vector.tensor_tensor(out=pr_[b][:], in0=pY[gi][:], in1=sg[b][:], op=ALU.mult),
                           reads=[pY_b[gi], sg_b[b]], writes=[pr_b[b]])
                    op("pool", lambda: nc.gpsimd.tensor_tensor(out=s2[0][:], in0=pr_[0][:], in1=pr_[1][:], op=ALU.add), reads=[pr_b[0], pr_b[1]], writes=[s2_b[0]])
                    op("pool", lambda: nc.gpsimd.tensor_tensor(out=s2[1][:], in0=pr_[2][:], in1=pr_[3][:], op=ALU.add), reads=[pr_b[2], pr_b[3]], writes=[s2_b[1]])
                    op("pool", lambda fc=fc: nc.gpsimd.tensor_tensor(out=mg[sl][:, fc, :], in0=s2[0][:], in1=s2[1][:], op=ALU.add),
                       reads=[s2_b[0], s2_b[1]], writes=[mg_b[sl]])
                dma("sp", self.mg_d[s, :, :, qt * 512:(qt + 1) * 512].rearrange("k p t -> p k t"), mg[sl][:], mg_b[sl], reads=[mg_b[sl]])

    def phase_MO(self, l, xin):
        nc, tr, W = self.nc, self.tr, self.W
        S, NSEQ, NQT = self.S, self.NSEQ, self.NQT
        op, dma = tr.op, tr.dma
        with ExitStack() as es:
            sb, ps = self.attn_common(es)
            wo = sb("wo", [128, 8, D], BF16); wo_b = Buf("wo")
            wfg = sb("wfg", [128, 8, DFF], BF16); wfu = sb("wfu", [128, 8, DFF], BF16)
            FG = [(0, 6), (6, 12), (12, 17), (17, 22)]
            wfg_b = [Buf("wfg") for _ in FG]
            dma("pool", wo[:], W["w_o"][l].rearrange("(n p) d -> p n d", p=128), wo_b, writes=[wo_b])
            for gi_, (f0, f1) in enumerate(FG):
                dma("pool", wfg[:, :, f0 * 128:f1 * 128], W["w_ffn_gate"][l, :, f0 * 128:f1 * 128].rearrange("(n p) d -> p n d", p=128), wfg_b[gi_], writes=[wfg_b[gi_]])
                dma("pool", wfu[:, :, f0 * 128:f1 * 128], W["w_ffn_up"][l, :, f0 * 128:f1 * 128].rearrange("(n p) d -> p n d", p=128), wfg_b[gi_], writes=[wfg_b[gi_]])

            def wf_buf(fc):
                for gi_, (f0, f1) in enumerate(FG):
                    if f0 <= fc < f1:
                        return wfg_b[gi_]
            g2 = sb("g2", [128, D], F32); const_b = Buf("const")
            ident = sb("ident", [128, 128], BF16)
            dma("sp", g2[:], W["norm2_g"][l].partition_broadcast(128), const_b, writes=[const_b])
            dma("sp", ident[:], self.ident_d, const_b, writes=[const_b])
            mgt = [sb("mgt", [128, 8, 512], BF16) for _ in range(2)]; mgt_b = [Buf("mgt") for _ in range(2)]
            xs = [sb("xs", [128, D], F32) for _ in range(2)]; xs_b = [Buf("xs") for _ in range(2)]
            xn = [sb("xn", [128, D], F32) for _ in range(2)]; xn_b = [Buf("xn") for _ in range(2)]
            junk = sb("junk", [128, D], F32); junk_b = Buf("junk")
            st1 = [sb("st1", [128, 4], F32) for _ in range(2)]; st1_b = [Buf("st1") for _ in range(2)]
            hb = [sb("hb", [128, D], BF16) for _ in range(2)]; hb_b = [Buf("hb") for _ in range(2)]
            hfT = [sb("hfT", [128, 8, 512], BF16) for _ in range(2)]; hfT_b = [Buf("hfT") for _ in range(2)]
            sl_ = [sb("sl", [128, 512], F32) for _ in range(2)]; sl_b = [Buf("sl") for _ in range(2)]
            at = [sb("at", [128, 2, 512], BF16) for _ in range(2)]; at_b = [Buf("at") for _ in range(2)]
            pO = ps("pO", [128, 1024], F32); pO_b = PBuf("pO")
            ptr = ps("ptr", [128, 8, 128], BF16); ptr_b = PBuf("ptr")
            pG = [ps("pG", [128, 512], F32) for _ in range(2)]; pG_b = [PBuf("pG") for _ in range(2)]
            pU = [ps("pU", [128, 512], F32) for _ in range(2)]; pU_b = [PBuf("pU") for _ in range(2)]
            tiles = [(s, qt) for s in range(NSEQ) for qt in range(NQT)]
            c = {"x": 0, "g": 0, "a": 0}

            def load_mg(ti):
                s, qt = tiles[ti]
                dma("sp", mgt[ti % 2][:], self.mg_d[s, :, :, qt * 512:(qt + 1) * 512].rearrange("k p t -> p k t"), mgt_b[ti % 2], writes=[mgt_b[ti % 2]])

            def load_x(ti, sub):
                s, qt = tiles[ti]
                xi = (ti * 4 + sub) % 2
                t0 = qt * 512 + sub * 128
                dma("sp", xs[xi][:], xin[s, t0:t0 + 128, :], xs_b[xi], writes=[xs_b[xi]])

            load_mg(0)
            load_x(0, 0)
            for ti in range(len(tiles)):
                s, qt = tiles[ti]
                sl = ti % 2
                if ti + 1 < len(tiles):
                    load_mg(ti + 1)
                pend_tr = []
                for sub in range(4):
                    xi = (ti * 4 + sub) % 2
                    t0 = qt * 512 + sub * 128
                    if sub < 3:
                        load_x(ti, sub + 1)
                    elif ti + 1 < len(tiles):
                        load_x(ti + 1, 0)
                    for half in range(2):
                        for kc in range(8):
                            op("pe", lambda kc=kc, half=half, sub=sub: nc.tensor.matmul(pO[:, half * 512:(half + 1) * 512], lhsT=mgt[sl][:, kc, sub * 128:(sub + 1) * 128],
                                                                                       rhs=wo[:, kc, half * 512:(half + 1) * 512], start=(kc == 0), stop=(kc == 7)),
                               reads=[mgt_b[sl], wo_b], writes=[pO_b], inc=(kc == 7))
                    op("dve", lambda xi=xi: nc.vector.tensor_tensor(out=xn[xi][:], in0=pO[:], in1=xs[xi][:], op=ALU.add), reads=[pO_b, xs_b[xi]], writes=[xn_b[xi]])
                    dma("sp", self.xm_d[s, t0:t0 + 128, :], xn[xi][:], xn_b[xi], reads=[xn_b[xi]])
                    op("act", lambda xi=xi: nc.scalar.activation(out=junk[:], in_=xn[xi][:], func=AF.Square, accum_out=st1[xi][:, 0:1]),
                       reads=[xn_b[xi]], writes=[junk_b, st1_b[xi]])
                    op("act", lambda xi=xi: nc.scalar.activation(out=st1[xi][:, 1:2], in_=st1[xi][:, 0:1], func=AF.Sqrt, bias=EPS, scale=1.0 / D),
                       reads=[st1_b[xi]], writes=[st1_b[xi]])
                    op("dve", lambda xi=xi: nc.vector.reciprocal(out=st1[xi][:, 2:3], in_=st1[xi][:, 1:2]), reads=[st1_b[xi]], writes=[st1_b[xi]])
                    op("dve", lambda xi=xi: nc.vector.scalar_tensor_tensor(out=hb[xi][:], in0=xn[xi][:], scalar=st1[xi][:, 2:3], in1=g2[:], op0=ALU.mult, op1=ALU.mult),
                       reads=[xn_b[xi], st1_b[xi], const_b], writes=[hb_b[xi]])
                    def tr_fn(xi=xi, sub=sub):
                        for kc in range(8):
                            op("pe", lambda kc=kc: nc.tensor.transpose(ptr[:, kc, :], hb[xi][:, kc * 128:(kc + 1) * 128], ident[:]),
                               reads=[hb_b[xi], const_b], writes=[ptr_b], inc=(kc == 7))
                        op("act", lambda: nc.scalar.copy(out=hfT[sl][:, :, sub * 128:(sub + 1) * 128], in_=ptr[:]), reads=[ptr_b], writes=[hfT_b[sl]])
                    if pend_tr:
                        pend_tr.pop()()
                    pend_tr.append(tr_fn)
                pend_tr.pop()()
                for fc in range(22):
                    gi = c["g"] % 2; c["g"] += 1
                    for kc in range(8):
                        op("pe", lambda kc=kc, gi=gi, fc=fc: nc.tensor.matmul(pG[gi][:], lhsT=wfg[:, kc, fc * 128:(fc + 1) * 128], rhs=hfT[sl][:, kc, :],
                                                                             start=(kc == 0), stop=(kc == 7)),
                           reads=[wf_buf(fc), hfT_b[sl]], writes=[pG_b[gi]], inc=(kc == 7))
                    for kc in range(8):
                        op("pe", lambda kc=kc, gi=gi, fc=fc: nc.tensor.matmul(pU[gi][:], lhsT=wfu[:, kc, fc * 128:(fc + 1) * 128], rhs=hfT[sl][:, kc, :],
                                                                             start=(kc == 0), stop=(kc == 7)),
                           reads=[wf_buf(fc), hfT_b[sl]], writes=[pU_b[gi]], inc=(kc == 7))
                    op("act", lambda gi=gi: nc.scalar.activation(out=sl_[gi][:], in_=pG[gi][:], func=AF.Silu), reads=[pG_b[gi]], writes=[sl_b[gi]])
                    ai = (fc // 2) % 2
                    op("dve", lambda gi=gi, ai=ai, fc=fc: nc.vector.tensor_tensor(out=at[ai][:, fc % 2, :], in0=pU[gi][:], in1=sl_[gi][:], op=ALU.mult),
                       reads=[pU_b[gi], sl_b[gi]], writes=[at_b[ai]])
                    if fc % 2 == 1:
                        f0 = fc - 1
                        dma("sp", self.aT_d[s, f0:f0 + 2, :, qt * 512:(qt + 1) * 512].rearrange("k p t -> p k t"), at[ai][:], at_b[ai], reads=[at_b[ai]])

    def phase_M2d(self, l, xout):
        nc, tr, W = self.nc, self.tr, self.W
        S, NSEQ, NQT = self.S, self.NSEQ, self.NQT
        op, dma = tr.op, tr.dma
        with ExitStack() as es:
            sb, ps = self.attn_common(es)
            wd = sb("wd", [128, 22, D], BF16)
            DG = [(0, 6), (6, 12), (12, 17), (17, 22)]
            wd_bs = [Buf("wd") for _ in DG]
            for gi_, (f0, f1) in enumerate(DG):
                dma("pool", wd[:, f0:f1, :], W["w_ffn_down"][l, f0 * 128:f1 * 128, :].rearrange("(n p) d -> p n d", p=128), wd_bs[gi_], writes=[wd_bs[gi_]])

            def wd_buf(fc):
                for gi_, (f0, f1) in enumerate(DG):
                    if f0 <= fc < f1:
                        return wd_bs[gi_]
            aTt = [sb("aTt", [128, 22, 512], BF16) for _ in range(2)]; aT_b = [Buf("aTt") for _ in range(2)]
            xs = [sb("xs", [128, D], F32) for _ in range(2)]; xs_b = [Buf("xs") for _ in range(2)]
            xo = [sb("xo", [128, D], F32) for _ in range(2)]; xo_b = [Buf("xo") for _ in range(2)]
            pO = [ps("pO", [128, 1024], F32) for _ in range(2)]; pO_b = [PBuf("pO") for _ in range(2)]
            tiles = [(s, qt) for s in range(NSEQ) for qt in range(NQT)]

            def load_a(ti):
                s, qt = tiles[ti]
                dma("sp", aTt[ti % 2][:], self.aT_d[s, :, :, qt * 512:(qt + 1) * 512].rearrange("k p t -> p k t"), aT_b[ti % 2], writes=[aT_b[ti % 2]])

            def load_x(ti, sub):
                s, qt = tiles[ti]
                xi = (ti * 4 + sub) % 2
                t0 = qt * 512 + sub * 128
                dma("sp", xs[xi][:], self.xm_d[s, t0:t0 + 128, :], xs_b[xi], writes=[xs_b[xi]])

            load_a(0)
            load_x(0, 0)
            for ti in range(len(tiles)):
                s, qt = tiles[ti]
                sl = ti % 2
                if ti + 1 < len(tiles):
                    load_a(ti + 1)
                for sub in range(4):
                    xi = (ti * 4 + sub) % 2
                    t0 = qt * 512 + sub * 128
                    if sub < 3:
                        load_x(ti, sub + 1)
                    elif ti + 1 < len(tiles):
                        load_x(ti + 1, 0)
                    for half in range(2):
                        for fc in range(22):
                            op("pe", lambda fc=fc, half=half, sub=sub, xi=xi: nc.tensor.matmul(pO[xi][:, half * 512:(half + 1) * 512], lhsT=aTt[sl][:, fc, sub * 128:(sub + 1) * 128],
                                                                                              rhs=wd[:, fc, half * 512:(half + 1) * 512], start=(fc == 0), stop=(fc == 21)),
                               reads=[aT_b[sl], wd_buf(fc)], writes=[pO_b[xi]], inc=(fc == 21))
                    op("dve", lambda xi=xi: nc.vector.tensor_tensor(out=xo[xi][:], in0=pO[xi][:], in1=xs[xi][:], op=ALU.add), reads=[pO_b[xi], xs_b[xi]], writes=[xo_b[xi]])
                    dma("sp", xout[s, t0:t0 + 128, :], xo[xi][:], xo_b[xi], reads=[xo_b[xi]])


def make_consts(S):
    bf = ml_dtypes.bfloat16
    pos = np.arange(S, dtype=np.float32)
    out = {}
    for half, name in ((32, "cs32"), (16, "cs16")):
        inv = np.power(np.float32(10000.0), -np.arange(half, dtype=np.float32) / np.float32(half)).astype(np.float32)
        ang = pos[:, None] * inv[None, :]
        out[name] = np.concatenate([np.cos(ang), np.sin(ang)], axis=1).astype(np.float32)
    masks = np.zeros((128, MASK_W), dtype=np.float32)
    k = np.arange(128)[:, None]
    for mk, (O, Wd, R_, dil) in MASK_DEF.items():
        j = np.arange(Wd)[None, :]
        d = k - j + O
        masks[:, MASK_COL[mk]:MASK_COL[mk] + Wd] = ((np.abs(d) <= R_) & (d % dil == 0)).astype(np.float32)
    out["masks"] = masks.astype(bf)
    out["ident"] = np.eye(128, dtype=np.float32).astype(bf)
    return out


_CACHE = {}
WEIGHT_NAMES = ["norm1_g", "w_in", "w_gate", "a_qnorm_g", "a_knorm_g", "a_lambda", "a_subln_g", "b_qnorm_g", "b_knorm_g",
                "c_qa_norm_g", "c_kva_norm_g", "c_w_uq", "c_w_ukv", "c_qnorm_g", "c_knorm_g", "d_qnorm_g", "d_knorm_g", "d_sink",
                "w_br_a", "w_br_b", "w_br_c", "w_br_d", "w_o", "norm2_g", "w_ffn_gate", "w_ffn_up", "w_ffn_down"]


def kernel(**inputs):
    xp = np.ascontiguousarray(np.asarray(inputs["x_prompt"], dtype=np.float32))
    xs = np.ascontiguousarray(np.asarray(inputs["x_sample"], dtype=np.float32))
    S = xp.shape[1]
    seqs = [xp[i] for i in range(xp.shape[0])] + [xs[i] for i in range(xs.shape[0])]
    n = 8
    doubles = [0, 1, 4, 5]
    assign = [[c, (8 + doubles.index(c)) if c in doubles else -1] for c in range(n)]
    key = ("full", S)
    if key not in _CACHE:
        _CACHE[key] = Builder(S=S, NSEQ=2, NL=2)
    b = _CACHE[key]
    consts = make_consts(S)
    wmap = {k: np.ascontiguousarray(np.asarray(inputs[k], dtype=np.float32)) for k in WEIGHT_NAMES}
    in_maps = []
    for c in range(n):
        m = dict(wmap)
        m.update(consts)
        second = seqs[assign[c][1]] if assign[c][1] >= 0 else np.zeros_like(seqs[0])
        m["x"] = np.stack([seqs[assign[c][0]], second], axis=0)
        in_maps.append(m)
    res = run_bass_kernel_spmd(b.nc, in_maps, core_ids=list(range(n)))
    outs = [None] * 12
    for c in range(n):
        y = np.asarray(res.results[c]["y"], dtype=np.float32)
        outs[assign[c][0]] = y[0]
        if assign[c][1] >= 0:
            outs[assign[c][1]] = y[1]
    y_prompt = np.stack(outs[0:4], axis=0).astype(np.float32)
    y_sample = np.stack(outs[4:12], axis=0).astype(np.float32)
    return (y_prompt, y_sample)
```
